# Optimizing a Trainium2 kernel written in Bass

```python
import math
import jax, jax.numpy as jnp
from jax import lax
import numpy as np


D_MODEL = 1024
BATCH = 4
SEQ = 8192
DEPTH = 2

CHUNK = 64
CONV_WIDTH = 4
RMS_EPS = 1e-6
GDN_HEADS = D_MODEL // 256
GDN_DK = 128
GDN_DV = 128
GDN_QK = GDN_HEADS * GDN_DK
GDN_V = GDN_HEADS * GDN_DV
SSD_HEADDIM = 64
SSD_HEADS = D_MODEL // 128
SSD_INNER = SSD_HEADS * SSD_HEADDIM
SSD_GROUPS = 2
SSD_STATE = 128
SSD_BC = SSD_GROUPS * SSD_STATE
LRU_WIDTH = D_MODEL // 2
LRU_BLOCKS = 8
LRU_BLOCK = LRU_WIDTH // LRU_BLOCKS
LRU_C = 8.0
N_BRANCH = 3
BRANCH_WIDTH = 512
D_FF = 4 * D_MODEL
N_MOD = 6
IN_SPLITS = (GDN_QK, GDN_QK, GDN_V, GDN_V, GDN_HEADS, GDN_HEADS,
             SSD_INNER, SSD_INNER, SSD_BC, SSD_BC, SSD_HEADS,
             LRU_WIDTH, LRU_WIDTH,
             N_BRANCH * D_MODEL)
D_IN = sum(IN_SPLITS)

kernel_name = 'hybrid_gdn_ssd_rglru_adaln_block'


def rmsnorm(x, w):
    xf = x.astype(jnp.float32)
    y = xf * lax.rsqrt(jnp.mean(xf * xf, axis=-1, keepdims=True) + RMS_EPS)
    return (y * w.astype(jnp.float32)).astype(x.dtype)


def l2norm(x):
    return x * lax.rsqrt(jnp.sum(x * x, axis=-1, keepdims=True) + RMS_EPS)


def split_cols(t, sizes):
    idx = np.cumsum(sizes)[:-1].tolist()
    return jnp.split(t, idx, axis=-1)


def causal_conv(x, w):
    width = w.shape[0]
    seq = x.shape[1]
    xp = jnp.pad(x, ((0, 0), (width - 1, 0), (0, 0)))
    return sum(xp[:, k:k + seq] * w[k] for k in range(width))


def gated_deltanet(q, k, v, z, b_raw, a_raw, a_log, dt_bias, norm_w):
    f32 = jnp.float32
    bsz, seq, _ = q.shape
    nc = seq // CHUNK

    def heads(t, d):
        return t.astype(f32).reshape(bsz, nc, CHUNK, GDN_HEADS, d).transpose(0, 1, 3, 2, 4)

    def per_head(t):
        return t.astype(f32).reshape(bsz, nc, CHUNK, GDN_HEADS).transpose(0, 1, 3, 2)

    q = l2norm(heads(q, GDN_DK)) * (GDN_DK ** -0.5)
    k = l2norm(heads(k, GDN_DK))
    v = heads(v, GDN_DV)
    beta = jax.nn.sigmoid(per_head(b_raw))
    g = -jnp.exp(a_log.astype(f32))[:, None] * jax.nn.softplus(per_head(a_raw) + dt_bias.astype(f32)[:, None])
    gcum = jnp.cumsum(g, axis=-1)
    causal = jnp.tril(jnp.ones((CHUNK, CHUNK), bool))
    strict = jnp.tril(jnp.ones((CHUNK, CHUNK), bool), -1)
    decay = jnp.exp(jnp.where(causal, gcum[..., :, None] - gcum[..., None, :], -jnp.inf))
    kk = jnp.einsum('bnhcd,bnhed->bnhce', k, k)
    m = jnp.where(strict, beta[..., :, None] * kk * decay, 0.0)
    eye = jnp.eye(CHUNK, dtype=f32)
    rhs = jnp.concatenate([beta[..., None] * v, (beta * jnp.exp(gcum))[..., None] * k], axis=-1)
    sol = lax.linalg.triangular_solve(eye + m, rhs, left_side=True, lower=True, unit_diagonal=True)
    u, w = sol[..., :GDN_DV], sol[..., GDN_DV:]
    qk = jnp.einsum('bnhcd,bnhed->bnhce', q, k) * decay
    q_dec = q * jnp.exp(gcum)[..., None]
    k_dec = k * jnp.exp(gcum[..., -1:] - gcum)[..., None]
    g_tot = jnp.exp(gcum[..., -1])

    def step(state, inp):
        u_c, w_c, qk_c, qd_c, kd_c, gt_c = inp
        v_new = u_c - jnp.einsum('bhck,bhkv->bhcv', w_c, state)
        o_c = jnp.einsum('bhck,bhkv->bhcv', qd_c, state) + jnp.einsum('bhce,bhev->bhcv', qk_c, v_new)
        state = state * gt_c[..., None, None] + jnp.einsum('bhck,bhcv->bhkv', kd_c, v_new)
        return state, o_c

    xs = tuple(jnp.moveaxis(t, 1, 0) for t in (u, w, qk, q_dec, k_dec, g_tot))
    s0 = jnp.zeros((bsz, GDN_HEADS, GDN_DK, GDN_DV), f32)
    _, o = lax.scan(step, s0, xs)
    o = o.transpose(1, 0, 3, 2, 4).reshape(bsz, seq, GDN_HEADS, GDN_DV)
    zh = z.astype(f32).reshape(bsz, seq, GDN_HEADS, GDN_DV)
    o = rmsnorm(o, norm_w) * jax.nn.silu(zh)
    return o.reshape(bsz, seq, GDN_V)


def ssd_scan(xs, bm, cm, dt_raw, a_log, dt_bias, d_skip):
    f32 = jnp.float32
    bsz, seq, _ = xs.shape
    nc = seq // CHUNK
    hpg = SSD_HEADS // SSD_GROUPS
    x = xs.astype(f32).reshape(bsz, nc, CHUNK, SSD_GROUPS, hpg, SSD_HEADDIM)
    bm = bm.astype(f32).reshape(bsz, nc, CHUNK, SSD_GROUPS, SSD_STATE)
    cm = cm.astype(f32).reshape(bsz, nc, CHUNK, SSD_GROUPS, SSD_STATE)
    dt = jax.nn.softplus(dt_raw.astype(f32) + dt_bias.astype(f32)).reshape(bsz, nc, CHUNK, SSD_GROUPS, hpg)
    a = -jnp.exp(a_log.astype(f32)).reshape(SSD_GROUPS, hpg)
    acum = jnp.cumsum(dt * a, axis=2)
    xdt = x * dt[..., None]
    causal = jnp.tril(jnp.ones((CHUNK, CHUNK), bool))
    seg = acum[:, :, :, None] - acum[:, :, None, :]
    lmat = jnp.exp(jnp.where(causal[:, :, None, None], seg, -jnp.inf))
    cb = jnp.einsum('bncgs,bnegs->bnceg', cm, bm)
    y_diag = jnp.einsum('bnceg,bncegh,bneghp->bncghp', cb, lmat, xdt)
    decay_end = jnp.exp(acum[:, :, -1:] - acum)
    chunk_states = jnp.einsum('bncgs,bncgh,bncghp->bnghps', bm, decay_end, xdt)
    chunk_decay = jnp.exp(acum[:, :, -1])

    def step(state, inp):
        st, dc = inp
        return state * dc[..., None, None] + st, state

    s0 = jnp.zeros((bsz, SSD_GROUPS, hpg, SSD_HEADDIM, SSD_STATE), f32)
    _, prev = lax.scan(step, s0, (jnp.moveaxis(chunk_states, 1, 0), jnp.moveaxis(chunk_decay, 1, 0)))
    prev = jnp.moveaxis(prev, 0, 1)
    y_off = jnp.einsum('bncgs,bnghps,bncgh->bncghp', cm, prev, jnp.exp(acum))
    y = y_diag + y_off + d_skip.astype(f32).reshape(SSD_GROUPS, hpg)[:, :, None] * x
    return y.reshape(bsz, seq, SSD_INNER)


def rg_lru(x, w_a, b_a, w_x, b_x, lam):
    f32 = jnp.float32
    bsz, seq, _ = x.shape
    xf = x.astype(f32)
    xb = xf.reshape(bsz, seq, LRU_BLOCKS, LRU_BLOCK)
    r = jax.nn.sigmoid(jnp.einsum('bsnd,nde->bsne', xb, w_a.astype(f32)).reshape(bsz, seq, LRU_WIDTH) + b_a.astype(f32))
    i = jax.nn.sigmoid(jnp.einsum('bsnd,nde->bsne', xb, w_x.astype(f32)).reshape(bsz, seq, LRU_WIDTH) + b_x.astype(f32))
    log_a = -LRU_C * r * jax.nn.softplus(-lam.astype(f32))
    a = jnp.exp(log_a)
    u = jnp.sqrt(-jnp.expm1(2.0 * log_a)) * (i * xf)

    def combine(left, right):
        a_l, u_l = left
        a_r, u_r = right
        return a_l * a_r, a_r * u_l + u_r

    _, hs = lax.associative_scan(combine, (a, u), axis=1)
    return hs


def hybrid_mixer(h, w_in, gdn_conv_w, gdn_a_log, gdn_dt_bias, gdn_norm,
                 ssd_conv_w, ssd_conv_b, ssd_a_log, ssd_dt_bias, ssd_d, ssd_norm,
                 lru_conv_w, lru_conv_b, lru_w_a, lru_b_a, lru_w_x, lru_b_x, lru_lambda,
                 w_branch, w_out):
    bsz, seq, _ = h.shape
    f32 = jnp.float32
    proj = h @ w_in
    (q, k, v, gdn_z, gdn_b, gdn_a, ssd_x, ssd_z, ssd_bm, ssd_cm, ssd_dt,
     lru_x, lru_gate, gate_logits) = split_cols(proj, IN_SPLITS)
    qkv = jax.nn.silu(causal_conv(jnp.concatenate([q, k, v], axis=-1), gdn_conv_w))
    q, k, v = split_cols(qkv, (GDN_QK, GDN_QK, GDN_V))
    y_a = gated_deltanet(q, k, v, gdn_z, gdn_b, gdn_a, gdn_a_log, gdn_dt_bias, gdn_norm)
    xbc = jax.nn.silu(causal_conv(jnp.concatenate([ssd_x, ssd_bm, ssd_cm], axis=-1), ssd_conv_w) + ssd_conv_b)
    sx, sb, sc = split_cols(xbc, (SSD_INNER, SSD_BC, SSD_BC))
    y = ssd_scan(sx, sb, sc, ssd_dt, ssd_a_log, ssd_dt_bias, ssd_d)
    gz = (y * jax.nn.silu(ssd_z.astype(f32))).reshape(bsz, seq, SSD_GROUPS, SSD_INNER // SSD_GROUPS)
    y_b = rmsnorm(gz, ssd_norm.reshape(SSD_GROUPS, SSD_INNER // SSD_GROUPS)).reshape(bsz, seq, SSD_INNER)
    xc = causal_conv(lru_x, lru_conv_w) + lru_conv_b
    y_c = rg_lru(xc, lru_w_a, lru_b_a, lru_w_x, lru_b_x, lru_lambda) * jax.nn.gelu(lru_gate.astype(f32))
    gates = jax.nn.sigmoid(gate_logits.reshape(bsz, seq, N_BRANCH, D_MODEL))
    merged = sum(gates[:, :, r] * (y_r.astype(h.dtype) @ w_branch[r]) for r, y_r in enumerate((y_a, y_b, y_c)))
    return merged @ w_out


def setup_inputs(seed: int = 0) -> dict:
    key = jax.random.key(seed)
    ks = iter(jax.random.split(key, 40))
    L = DEPTH

    def nrm(shape, scale):
        return jax.random.normal(next(ks), shape, jnp.float32) * scale

    def gain(shape):
        return 1.0 + nrm(shape, 0.1)

    def dt_bias_init(n):
        dt = jnp.exp(jax.random.uniform(next(ks), (L, n), jnp.float32, math.log(1e-3), math.log(1e-1)))
        return dt + jnp.log(-jnp.expm1(-dt))

    def a_log_init(n):
        return jnp.log(jax.random.uniform(next(ks), (L, n), jnp.float32, 1.0, 16.0))

    x = nrm((BATCH, SEQ, D_MODEL), 1.0)
    c = nrm((BATCH, D_MODEL), 1.0)
    ada_w = nrm((L, D_MODEL, N_MOD * D_MODEL), 0.3 * D_MODEL ** -0.5)
    ada_b = nrm((L, N_MOD * D_MODEL), 0.02)
    norm_mix = gain((L, D_MODEL))
    w_in = nrm((L, D_MODEL, D_IN), D_MODEL ** -0.5)
    gdn_conv_w = nrm((L, CONV_WIDTH, 2 * GDN_QK + GDN_V), CONV_WIDTH ** -0.5)
    gdn_a_log = a_log_init(GDN_HEADS)
    gdn_dt_bias = dt_bias_init(GDN_HEADS)
    gdn_norm = gain((L, GDN_DV))
    ssd_conv_w = nrm((L, CONV_WIDTH, SSD_INNER + 2 * SSD_BC), CONV_WIDTH ** -0.5)
    ssd_conv_b = nrm((L, SSD_INNER + 2 * SSD_BC), 0.02)
    ssd_a_log = a_log_init(SSD_HEADS)
    ssd_dt_bias = dt_bias_init(SSD_HEADS)
    ssd_d = gain((L, SSD_HEADS))
    ssd_norm = gain((L, SSD_INNER))
    lru_conv_w = nrm((L, CONV_WIDTH, LRU_WIDTH), CONV_WIDTH ** -0.5)
    lru_conv_b = nrm((L, LRU_WIDTH), 0.02)
    lru_w_a = nrm((L, LRU_BLOCKS, LRU_BLOCK, LRU_BLOCK), LRU_BLOCK ** -0.5)
    lru_b_a = nrm((L, LRU_WIDTH), 0.02)
    lru_w_x = nrm((L, LRU_BLOCKS, LRU_BLOCK, LRU_BLOCK), LRU_BLOCK ** -0.5)
    lru_b_x = nrm((L, LRU_WIDTH), 0.02)
    a_pow = jax.random.uniform(next(ks), (L, LRU_WIDTH), jnp.float32, 0.9, 0.999)
    s = a_pow ** (1.0 / LRU_C)
    lru_lambda = jnp.log(s) - jnp.log1p(-s)
    w_branch = nrm((L, N_BRANCH, BRANCH_WIDTH, D_MODEL), BRANCH_WIDTH ** -0.5)
    w_out = nrm((L, D_MODEL, D_MODEL), D_MODEL ** -0.5)
    norm_mlp = gain((L, D_MODEL))
    w_up = nrm((L, D_MODEL, D_FF), D_MODEL ** -0.5)
    w_down = nrm((L, D_FF, D_MODEL), D_FF ** -0.5)
    final_norm = gain((D_MODEL,))
    return {'x': x, 'c': c, 'ada_w': ada_w, 'ada_b': ada_b, 'norm_mix': norm_mix, 'w_in': w_in,
            'gdn_conv_w': gdn_conv_w, 'gdn_a_log': gdn_a_log, 'gdn_dt_bias': gdn_dt_bias, 'gdn_norm': gdn_norm,
            'ssd_conv_w': ssd_conv_w, 'ssd_conv_b': ssd_conv_b, 'ssd_a_log': ssd_a_log, 'ssd_dt_bias': ssd_dt_bias,
            'ssd_d': ssd_d, 'ssd_norm': ssd_norm,
            'lru_conv_w': lru_conv_w, 'lru_conv_b': lru_conv_b, 'lru_w_a': lru_w_a, 'lru_b_a': lru_b_a,
            'lru_w_x': lru_w_x, 'lru_b_x': lru_b_x, 'lru_lambda': lru_lambda,
            'w_branch': w_branch, 'w_out': w_out, 'norm_mlp': norm_mlp, 'w_up': w_up, 'w_down': w_down,
            'final_norm': final_norm}


def reference(x, c, ada_w, ada_b, norm_mix, w_in, gdn_conv_w, gdn_a_log, gdn_dt_bias, gdn_norm,
              ssd_conv_w, ssd_conv_b, ssd_a_log, ssd_dt_bias, ssd_d, ssd_norm,
              lru_conv_w, lru_conv_b, lru_w_a, lru_b_a, lru_w_x, lru_b_x, lru_lambda,
              w_branch, w_out, norm_mlp, w_up, w_down, final_norm):
    for l in range(DEPTH):
        mod = jax.nn.silu(c) @ ada_w[l] + ada_b[l]
        sh1, sc1, gt1, sh2, sc2, gt2 = jnp.split(mod[:, None, :], N_MOD, axis=-1)
        h = rmsnorm(x, norm_mix[l]) * (1 + sc1) + sh1
        mix = hybrid_mixer(h, w_in[l], gdn_conv_w[l], gdn_a_log[l], gdn_dt_bias[l], gdn_norm[l],
                           ssd_conv_w[l], ssd_conv_b[l], ssd_a_log[l], ssd_dt_bias[l], ssd_d[l], ssd_norm[l],
                           lru_conv_w[l], lru_conv_b[l], lru_w_a[l], lru_b_a[l], lru_w_x[l], lru_b_x[l],
                           lru_lambda[l], w_branch[l], w_out[l])
        x = x + gt1 * mix
        h = rmsnorm(x, norm_mlp[l]) * (1 + sc2) + sh2
        x = x + gt2 * (jnp.square(jax.nn.relu(h @ w_up[l])) @ w_down[l])
    return rmsnorm(x, final_norm)
```

```python
import contextlib
import os as _os
import numpy as np
import concourse.bass as bass
import concourse.mybir as mybir
from concourse.bass_utils import run_bass_kernel_spmd

F32 = mybir.dt.float32
BF16 = mybir.dt.bfloat16
AF = mybir.ActivationFunctionType
ALU = mybir.AluOpType

D = 1024
TT = 512
NU = 158
N_ADA = 96
EPS = 1e-6
GDN_IL = 2
NPV = 208
NCST = 5 * 128 + 12 * 128
PV_NM, PV_NMLP, PV_ADAB, PV_GCW, PV_SCW, PV_SCB, PV_LCW, PV_LCB = 0, 8, 16, 64, 112, 144, 152, 168
PV_LBA, PV_LBX, PV_LAM, PV_GNW, PV_SNW, PV_SD, PV_RB, PV_RA, PV_FNW = 172, 176, 180, 184, 185, 189, 193, 194, 195
C_ID, C_ONES, C_NEGUS, C_NEGUI, C_POSLS, C_SEL = 0, 128, 256, 384, 512, 640
SELROWS = [32, 33, 34, 35, 40, 41, 42, 43, 44, 45, 46, 47]

ENGS = ("pe", "act", "dve", "pool", "sp")
N_DMA_SEMS = 24


class Buf:
    __slots__ = ("name", "w", "r", "excl")

    def __init__(self, name="", excl=False):
        self.name = name
        self.w = None
        self.r = []
        self.excl = excl


class Sched:
    def __init__(self, nc):
        self.nc = nc
        self.ops = {e: [] for e in ENGS}
        self.cnt = {e: 0 for e in ENGS}
        self.waited = {e: {} for e in ENGS}
        self.dma_i = 0
        self.dma_cnt = [0] * N_DMA_SEMS
        self.sw_gen = {}
        self.n_ops = 0

    def _need(self, eng, tok, waits, is_raw):
        if tok is None:
            return
        semkey, val, teng = tok
        if teng == eng and semkey == eng and not is_raw and eng == "pe":
            return
        if self.waited[eng].get(semkey, 0) >= val:
            return
        if waits.get(semkey, 0) < val:
            waits[semkey] = val

    def op(self, eng, fn, reads=(), writes=(), dma=False, swslot=None):
        waits = {}
        for b in reads:
            self._need(eng, b.w, waits, True)
            if b.excl:
                for t in b.r:
                    if t[2] != eng:
                        self._need(eng, t, waits, False)
        for b in writes:
            self._need(eng, b.w, waits, False)
            for t in b.r:
                self._need(eng, t, waits, False)
        clear = None
        if swslot is not None:
            gen = self.sw_gen.get(swslot, 0)
            self.sw_gen[swslot] = gen + 1
            tok = ("sw%d:%d" % (swslot, gen), 16, "dma")
            inc = ("sw%d" % swslot, 16)
            if gen > 0:
                clear = "sw%d" % swslot
        elif dma:
            k = self.dma_i % N_DMA_SEMS
            self.dma_i += 1
            if self.dma_cnt[k] > 0:
                self._need(eng, ("dma%d" % k, 16 * self.dma_cnt[k], "dma"), waits, False)
            self.dma_cnt[k] += 1
            tok = ("dma%d" % k, 16 * self.dma_cnt[k], "dma")
            inc = ("dma%d" % k, 16)
        else:
            self.cnt[eng] += 1
            tok = (eng, self.cnt[eng], eng)
            inc = (eng, 1)
        for sk, v in waits.items():
            self.waited[eng][sk] = v
        self.ops[eng].append((fn, tuple(waits.items()), inc, clear))
        for b in reads:
            b.r.append(tok)
            if len(b.r) > 64:
                b.r = _compact(b.r)
        for b in writes:
            b.w = tok
            b.r = []
        self.n_ops += 1
        return tok

    def final_wait(self, eng="sp"):
        waits = {}
        for e in ENGS:
            if self.cnt[e] and e != eng:
                self._need(eng, (e, self.cnt[e], e), waits, False)
        for k in range(N_DMA_SEMS):
            if self.dma_cnt[k]:
                self._need(eng, ("dma%d" % k, 16 * self.dma_cnt[k], "dma"), waits, False)
        self.ops[eng].append((None, tuple(waits.items()), None, None))

    def emit(self):
        nc = self.nc
        with contextlib.ExitStack() as st:
            sems = {}
            for e in ENGS:
                sems[e] = st.enter_context(nc.semaphore("s_" + e))
            for k in range(N_DMA_SEMS):
                sems["dma%d" % k] = st.enter_context(nc.semaphore("s_dma%d" % k))
            for k in self.sw_gen:
                sems["sw%d" % k] = st.enter_context(nc.semaphore("s_sw%d" % k))
            block = st.enter_context(nc.Block())

            def run(eng_name):
                def body(eng):
                    for fn, waits, inc, clear in self.ops[eng_name]:
                        for sk, v in waits:
                            eng.wait_ge(sems[sk.split(":")[0]], v)
                        if fn is None:
                            continue
                        if clear is not None:
                            eng.wait_ge(sems[clear], 16)
                            eng.sem_clear(sems[clear])
                        ins = fn(eng)
                        ins.then_inc(sems[inc[0]], inc[1])
                return body

            block.tensor(run("pe"))
            block.scalar(run("act"))
            block.vector(run("dve"))
            block.gpsimd(run("pool"))
            block.sync(run("sp"))


def _compact(toks):
    best = {}
    for t in toks:
        if t[0] not in best or best[t[0]][1] < t[1]:
            best[t[0]] = t
    return list(best.values())


class Slot:
    __slots__ = ("pool", "i", "ap", "buf")


class SlotPool:
    def __init__(self, tile, n, name):
        self.tile = tile
        self.n = n
        self.free = list(range(n))
        self.bufs = [Buf("%s%d" % (name, i)) for i in range(n)]
        self.name = name
        self.minfree = n

    def alloc(self, idx=None):
        if not self.free:
            raise RuntimeError("pool %s exhausted" % self.name)
        if idx is None:
            i = self.free.pop(0)
        else:
            self.free.remove(idx)
            i = idx
        self.minfree = min(self.minfree, len(self.free))
        s = Slot()
        s.pool, s.i, s.ap, s.buf = self, i, self.tile[:, i, :], self.bufs[i]
        return s

    def release(self, s):
        assert s.i not in self.free
        self.free.append(s.i)


class _Stop(Exception):
    pass


def build_program(T, L, dbg=False, stop_after=99, order=None):
    assert T % TT == 0
    NT = T // TT
    nc = bass.Bass("TRN2", target_bir_lowering=False)
    x_d = nc.dram_tensor("x", [T, D], F32, kind="ExternalInput").ap()
    cT_d = nc.dram_tensor("cT", [128, 8], F32, kind="ExternalInput").ap()
    pv_d = nc.dram_tensor("pv", [L, 128, NPV], F32, kind="ExternalInput").ap()
    cst_d = nc.dram_tensor("cst", [128, NCST], F32, kind="ExternalInput").ap()
    wada_d = nc.dram_tensor("wada", [L * N_ADA, 128, 512], F32, kind="ExternalInput").ap()
    wp_d = nc.dram_tensor("wpack", [L * NU, 128, 1024], F32, kind="ExternalInput").ap()
    y_d = nc.dram_tensor("y", [T, D], F32, kind="ExternalOutput").ap()
    wbf_d = nc.dram_tensor("wbf", [L * NU, 128, 1024], BF16, kind="Internal").ap()
    b_wbf = [Buf("wbf%d" % i) for i in range(L * NU)]
    dbg_d = None
    if dbg:
        dbg_d = nc.dram_tensor("dbg", [128, 12, TT], F32, kind="ExternalOutput").ap()

    st = contextlib.ExitStack()
    with st:
        S = Sched(nc)

        def sbt(name, shape, dt=F32):
            return st.enter_context(nc.sbuf_tensor("sb_" + name, shape, dt))

        NBIG, NSM, NRP, NWS, NRAW, NST = 33, 40, 14, 6, 3, 3
        xT = sbt("xT", [128, 8, TT]); b_xT = [Buf("xT%d" % c) for c in range(8)]
        hT = sbt("hT", [128, 8, TT], BF16); b_hT = [Buf("hT%d" % c) for c in range(8)]
        yT = sbt("yT", [128, 12, TT], BF16); b_yT = [Buf("yT%d" % c) for c in range(12)]
        mT = sbt("mT", [128, 8, TT], BF16); b_mT = [Buf("mT%d" % c) for c in range(8)]
        wring = sbt("wring", [128, NWS, 1024], BF16); b_wr = [Buf("wr%d" % i) for i in range(NWS)]
        wstage = sbt("wstage", [128, NST, 1024]); b_ws = [Buf("ws%d" % i) for i in range(NST)]
        bigt = sbt("bigt", [128, NBIG, TT]); BP = SlotPool(bigt, NBIG, "big")
        smt = sbt("smt", [128, NSM, 128]); SP = SlotPool(smt, NSM, "sm")
        rpt = sbt("rpt", [128, NRP, 256]); RP = SlotPool(rpt, NRP, "rp")
        rawt = sbt("rawt", [128, NRAW, TT + 3]); RAWP = SlotPool(rawt, NRAW, "raw")
        lxb = sbt("lxb", [128, 4, TT], BF16); b_lxb = [Buf("lxb%d" % j) for j in range(4)]
        lbdt = sbt("lbdt", [128, 1024], BF16); b_lbd = Buf("lbd")
        cst = sbt("cst", [128, NCST]); b_cst = Buf("cst")
        pv = sbt("pv", [128, L, NPV]); b_pv = Buf("pv")
        cTt = sbt("cTt", [128, 8]); b_cT = Buf("cT")
        modt = sbt("modt", [128, L, 64]); b_mod = Buf("mod")
        Sg = sbt("Sg", [128, L * 4, 128]); b_Sg = [Buf("Sg%d" % i) for i in range(L * 4)]
        Ss = sbt("Ss", [128, L, 512]); b_Ss = [Buf("Ss%d" % i) for i in range(L)]
        hl = sbt("hl", [128, L * 4, 2]); b_hl = [Buf("hl%d" % i) for i in range(L * 4)]
        tails = sbt("tails", [128, L * 24, 3]); b_tl = [Buf("tl%d" % i) for i in range(L * 24)]
        pst = [st.enter_context(nc.psum_tensor("ps%d" % i, [128, 512], F32)) for i in range(8)]

        class PSPool:
            def __init__(self):
                self.free = list(range(8))
                self.bufs = [Buf("ps%d" % i, excl=True) for i in range(8)]

            def alloc(self):
                if not self.free:
                    raise RuntimeError("PSUM exhausted")
                i = self.free.pop(0)
                s = Slot()
                s.pool, s.i, s.ap, s.buf = self, i, pst[i][:], self.bufs[i]
                return s

            def release(self, s):
                assert s.i not in self.free
                self.free.append(s.i)

        PS = PSPool()

        ident = cst[:, C_ID:C_ID + 128]
        ones = cst[:, C_ONES:C_ONES + 128]
        NEGUS = cst[:, C_NEGUS:C_NEGUS + 128]
        NEGUI = cst[:, C_NEGUI:C_NEGUI + 128]
        POSLS = cst[:, C_POSLS:C_POSLS + 128]

        def sel(row):
            k = SELROWS.index(row)
            return cst[:, C_SEL + k * 128:C_SEL + (k + 1) * 128]

        def bl(x):
            return [y if isinstance(y, Buf) else y.buf for y in x]

        def PE_mm(out, lhsT, rhs, start, stop, reads, writes):
            S.op("pe", lambda e: e.matmul(out=out, lhsT=lhsT, rhs=rhs, start=start, stop=stop), bl(reads), bl(writes))

        def PE_tr(out, in_, idn, reads, writes):
            S.op("pe", lambda e: e.transpose(out=out, in_=in_, identity=idn), bl(reads) + [b_cst], bl(writes))

        def ACT(out, in_, func, reads, writes, **kw):
            S.op("act", lambda e: e.activation(out=out, in_=in_, func=func, **kw), bl(reads), bl(writes))

        def V(method, reads, writes, **kw):
            S.op("dve", lambda e: getattr(e, method)(**kw), bl(reads), bl(writes))

        def DMA(q, out, in_, reads, writes):
            S.op(q, lambda e: e.dma_start(out=out, in_=in_), bl(reads), bl(writes), dma=True)

        def pvc(l, off, n=1):
            return pv[:, l, off:off + n]

        class WStream:
            def __init__(self):
                self.recorded = []
                self.seq = None if order is None else [(ti, l, u) for ti in range(NT) for l in range(L) for u in order]
                self.issued = 0
                self.used = 0

            @staticmethod
            def width(u):
                if 38 <= u < 86 and (u - 38) % 2 == 0:
                    return 512
                return 1024

            def _issue(self, i, ti, l, u):
                s = i % NWS
                w = self.width(u)
                g = l * NU + u
                if ti == 0:
                    ss = i % NST
                    DMA("sp", wstage[:, ss, 0:w], wp_d[g, :, 0:w], [], [b_ws[ss]])
                    ceng = ("pool", "dve", "act")[i % 3]
                    if ceng == "act":
                        S.op("act", lambda e, s=s, ss=ss, w=w: e.activation(out=wring[:, s, 0:w], in_=wstage[:, ss, 0:w], func=AF.Copy), [b_ws[ss]], [b_wr[s]])
                    else:
                        S.op(ceng, lambda e, s=s, ss=ss, w=w: e.tensor_copy(out=wring[:, s, 0:w], in_=wstage[:, ss, 0:w]), [b_ws[ss]], [b_wr[s]])
                    if NT > 1:
                        DMA("sp", wbf_d[g, :, 0:w], wring[:, s, 0:w], [b_wr[s]], [b_wbf[g]])
                else:
                    DMA("sp", wring[:, s, 0:w], wbf_d[g, :, 0:w], [b_wbf[g]], [b_wr[s]])

            def get(self, l, u):
                i = self.used
                self.used += 1
                if self.seq is None:
                    self.recorded.append(u)
                    self._issue(i, 0, l, u)
                else:
                    assert self.seq[i][1:] == (l, u), (self.seq[i], l, u)
                    lim = min(i + NWS - 1, len(self.seq) - 1)
                    while self.issued <= lim:
                        self._issue(self.issued, *self.seq[self.issued])
                        self.issued += 1
                s = i % NWS
                return wring[:, s, :], b_wr[s]

        W = WStream()

        DMA("sp", cst[:], cst_d, [], [b_cst])
        DMA("sp", pv[:], pv_d.rearrange("l p n -> p l n"), [], [b_pv])
        DMA("sp", cTt[:], cT_d, [], [b_cT])
        V("memset", [], b_Sg, ap=Sg[:], constant=0.0)
        V("memset", [], b_Ss, ap=Ss[:], constant=0.0)
        V("memset", [], b_hl, ap=hl[:], constant=0.0)
        V("memset", [], b_tl, ap=tails[:], constant=0.0)
        V("memset", [], [b_mod], ap=modt[:], constant=0.0)
        ACT(cTt[:], cTt[:], AF.Silu, [b_cT], [b_cT])
        for l in range(L):
            pm = PS.alloc()
            for ob in range(48):
                for half in range(2):
                    ws = BP.alloc()
                    DMA("sp", ws.ap, wada_d[(l * 48 + ob) * 2 + half], [], [ws])
                    for cc in range(4):
                        c = half * 4 + cc
                        PE_mm(pm.ap[:, ob:ob + 1], ws.ap[:, cc * 128:(cc + 1) * 128], cTt[:, c:c + 1],
                              c == 0, c == 7, [ws, b_cT], [pm])
                    BP.release(ws)
            mo = SP.alloc()
            V("tensor_tensor", [pm, b_pv], [mo], out=mo.ap[:, 0:48], in0=pm.ap[:, 0:48], in1=pvc(l, PV_ADAB, 48), op=ALU.add)
            PS.release(pm)
            m = modt[:, l, :]
            V("scalar_tensor_tensor", [mo, b_pv], [b_mod], out=m[:, 0:8], in0=mo.ap[:, 8:16], scalar=1.0, in1=pvc(l, PV_NM, 8), op0=ALU.add, op1=ALU.mult)
            V("tensor_copy", [mo], [b_mod], out=m[:, 8:16], in_=mo.ap[:, 0:8])
            V("tensor_copy", [mo], [b_mod], out=m[:, 16:24], in_=mo.ap[:, 16:24])
            V("scalar_tensor_tensor", [mo, b_pv], [b_mod], out=m[:, 24:32], in0=mo.ap[:, 32:40], scalar=1.0, in1=pvc(l, PV_NMLP, 8), op0=ALU.add, op1=ALU.mult)
            V("tensor_copy", [mo], [b_mod], out=m[:, 32:40], in_=mo.ap[:, 24:32])
            V("tensor_copy", [mo], [b_mod], out=m[:, 40:48], in_=mo.ap[:, 40:48])
            SP.release(mo)
            ACT(m[:, 48:49], pvc(l, PV_RA), AF.Exp, [b_pv], [b_mod])
            V("tensor_scalar", [b_mod], [b_mod], out=m[:, 48:49], in0=m[:, 48:49], scalar1=-1.0, scalar2=None, op0=ALU.mult)
            ACT(m[:, 52:56], pvc(l, PV_LAM, 4), AF.Exp, [b_pv], [b_mod], scale=-1.0)
            ACT(m[:, 52:56], m[:, 52:56], AF.Ln, [b_mod], [b_mod], bias=1.0)
            V("tensor_scalar", [b_mod], [b_mod], out=m[:, 52:56], in0=m[:, 52:56], scalar1=-8.0, scalar2=None, op0=ALU.mult)

        def MOD(l, k):
            return modt[:, l, k * 8:(k + 1) * 8]

        def rstd_bc(scale, eps, srcs, src_bufs):
            pm = PS.alloc()
            n = len(srcs)
            for i, (a, b) in enumerate(zip(srcs, src_bufs)):
                sq = BP.alloc()
                ACT(sq.ap, a, AF.Square, [b], [sq])
                PE_mm(pm.ap, ones, sq.ap, i == 0, i == n - 1, [sq, b_cst], [pm])
                BP.release(sq)
            r = BP.alloc()
            ACT(r.ap, pm.ap, AF.Ln, [pm], [r], scale=scale, bias=eps)
            PS.release(pm)
            ACT(r.ap, r.ap, AF.Exp, [r], [r], scale=-0.5)
            return r

        xstat = {"pm": None}

        def stats_acc(c):
            if c == 0:
                assert xstat["pm"] is None
                xstat["pm"] = PS.alloc()
            pm = xstat["pm"]
            sq = BP.alloc()
            ACT(sq.ap, xT[:, c, :], AF.Square, [b_xT[c]], [sq])
            PE_mm(pm.ap, ones, sq.ap, c == 0, c == 7, [sq, b_cst], [pm])
            BP.release(sq)

        def x_rstd():
            pm = xstat["pm"]
            xstat["pm"] = None
            r = BP.alloc()
            ACT(r.ap, pm.ap, AF.Ln, [pm], [r], scale=1.0 / D, bias=EPS)
            PS.release(pm)
            ACT(r.ap, r.ap, AF.Exp, [r], [r], scale=-0.5)
            return r

        def norm_to_hT(l, kA, kB):
            r = x_rstd()
            A, Bc = MOD(l, kA), MOD(l, kB)
            for c in range(8):
                t = BP.alloc()
                V("tensor_tensor", [b_xT[c], r], [t], out=t.ap, in0=xT[:, c, :], in1=r.ap, op=ALU.mult)
                ACT(hT[:, c, :], t.ap, AF.Identity, [t, b_mod], [b_hT[c]], scale=A[:, c:c + 1], bias=Bc[:, c:c + 1])
                BP.release(t)
            BP.release(r)

        def proj(l, u, rhs_fn, rhs_bufs, nk=8):
            wap, wb = W.get(l, u)
            pm = PS.alloc()
            for c in range(nk):
                PE_mm(pm.ap, wap[:, c * 128:(c + 1) * 128], rhs_fn(c), c == 0, c == nk - 1, [wb, rhs_bufs[c]], [pm])
            return pm

        def inproj(l, u):
            return proj(l, u, lambda c: hT[:, c, :], b_hT)

        def conv(l, pm, tblk, wcol, bias, func, out):
            raw = RAWP.alloc()
            tb = b_tl[l * 24 + tblk]
            tl = tails[:, l * 24 + tblk, :]
            ACT(raw.ap[:, 3:TT + 3], pm.ap, AF.Copy, [pm], [raw])
            PS.release(pm)
            V("tensor_copy", [tb], [raw], out=raw.ap[:, 0:3], in_=tl)
            V("tensor_copy", [raw], [tb], out=tl, in_=raw.ap[:, TT:TT + 3])
            V("tensor_scalar", [raw, b_pv], [out], out=out.ap, in0=raw.ap[:, 0:TT], scalar1=wcol[:, 0:1], scalar2=None, op0=ALU.mult)
            for k in range(1, 4):
                V("scalar_tensor_tensor", [raw, b_pv, out], [out], out=out.ap, in0=raw.ap[:, k:TT + k], scalar=wcol[:, k:k + 1],
                  in1=out.ap, op0=ALU.mult, op1=ALU.add)
            RAWP.release(raw)
            if bias is None:
                ACT(out.ap, out.ap, func, [out], [out])
            else:
                ACT(out.ap, out.ap, func, [out, b_pv], [out], bias=bias)

        gstop = [int(_os.environ.get("GSTOP", "100000"))]

        def interleave(gens):
            gens = list(gens)
            while gens:
                for g in list(gens):
                    if gstop[0] <= 0:
                        raise _Stop()
                    gstop[0] -= 1
                    try:
                        next(g)
                    except StopIteration:
                        gens.remove(g)

        dbg_state = {"done": False}

        def layer(l):
            if stop_after <= 1:
                raise _Stop()
            norm_to_hT(l, 0, 1)
            m = modt[:, l, :]
            negA = m[:, 48:49]
            GC = BP.alloc(); EE = BP.alloc()
            TM = []

            def small_gen():
                pm = inproj(l, 0)
                RB = BP.alloc(); KD = BP.alloc(); CB = BP.alloc()
                for t_ in (RB, GC, EE, KD, CB):
                    V("memset", [], [t_], ap=t_.ap, constant=0.0)
                ACT(RB.ap[0:32, :], pm.ap[0:32, :], AF.Sigmoid, [pm], [RB])
                sl = slice(32, 64)
                ACT(CB.ap[sl, :], pm.ap[sl, :], AF.Exp, [pm, b_pv], [CB], bias=pv[sl, l, PV_RB:PV_RB + 1])
                PS.release(pm)
                yield
                ACT(CB.ap[sl, :], CB.ap[sl, :], AF.Ln, [CB], [CB], bias=1.0)
                V("tensor_scalar", [CB, b_mod], [KD], out=KD.ap[sl, :], in0=CB.ap[sl, :], scalar1=negA[sl, :], scalar2=None, op0=ALU.mult)
                for ci in range(4):
                    cs = slice(ci * 128, (ci + 1) * 128)
                    V("tensor_tensor_scan", [KD, b_cst], [GC], out=GC.ap[sl, cs], data0=ones[sl, :], data1=KD.ap[sl, cs],
                      initial=0.0, op0=ALU.mult, op1=ALU.add)
                yield
                ACT(EE.ap[sl, :], GC.ap[sl, :], AF.Exp, [GC], [EE])
                for ci in range(4):
                    cs = slice(ci * 128, (ci + 1) * 128)
                    ACT(KD.ap[sl, cs], GC.ap[sl, cs], AF.Exp, [GC], [KD], scale=-1.0, bias=GC.ap[sl, ci * 128 + 127:ci * 128 + 128])
                V("tensor_copy", [EE], [CB], out=CB.ap[32:36, :], in_=EE.ap[32:36, :])
                yield
                for ci in range(4):
                    cs = slice(ci * 128, (ci + 1) * 128)
                    pt = PS.alloc()
                    for k, src in enumerate((RB, CB, KD, GC)):
                        PE_tr(pt.ap[:, k * 128:(k + 1) * 128], src.ap[:, cs], ident, [src], [pt])
                    tm = RP.alloc()
                    ACT(tm.ap[:, 0:192].rearrange("p (b c) -> p b c", b=4), pt.ap.rearrange("p (b c) -> p b c", b=4)[:, :, 0:48], AF.Copy, [pt], [tm])
                    PS.release(pt)
                    TM.append(tm)
                    if ci == 1:
                        yield
                BP.release(RB); BP.release(CB); BP.release(KD)


            if stop_after <= 2:
                raise _Stop()
            def gdn_head(h):
                hs = l * 4 + h
                qc = BP.alloc(); kc = BP.alloc(); vc = BP.alloc()
                ub, ust = 1 + 4 * h, 1
                pmq = inproj(l, ub)
                conv(l, pmq, h, pv[:, l, PV_GCW + 4 * h:PV_GCW + 4 * h + 4], None, AF.Silu, qc)
                yield
                pmk = inproj(l, ub + ust)
                conv(l, pmk, 4 + h, pv[:, l, PV_GCW + 4 * (4 + h):PV_GCW + 4 * (4 + h) + 4], None, AF.Silu, kc)
                yield
                pmv = inproj(l, ub + 2 * ust)
                conv(l, pmv, 8 + h, pv[:, l, PV_GCW + 4 * (8 + h):PV_GCW + 4 * (8 + h) + 4], None, AF.Silu, vc)
                yield
                rq = rstd_bc(1.0, EPS, [qc.ap], [qc.buf])
                V("scalar_tensor_tensor", [qc, rq], [qc], out=qc.ap, in0=qc.ap, scalar=128.0 ** -0.5, in1=rq.ap, op0=ALU.mult, op1=ALU.mult)
                BP.release(rq)
                yield
                rk = rstd_bc(1.0, EPS, [kc.ap], [kc.buf])
                V("tensor_tensor", [kc, rk], [kc], out=kc.ap, in0=kc.ap, in1=rk.ap, op=ALU.mult)
                BP.release(rk)
                yield
                pe_ = PS.alloc()
                PE_mm(pe_.ap, sel(32 + h), EE.ap, True, True, [EE, b_cst], [pe_])
                qd = BP.alloc()
                if _os.environ.get("SKIPQD") is None:
                    V("tensor_tensor", [qc, pe_], [qd], out=qd.ap, in0=qc.ap, in1=pe_.ap, op=ALU.mult)
                gts = SP.alloc()
                V("tensor_copy", [pe_], [gts], out=gts.ap[:, 0:4], in_=pe_.ap.rearrange("p (c t) -> p c t", c=4)[:, :, 127])
                PS.release(pe_)
                oT = BP.alloc()
                yield
                for ci in range(4):
                    cs = slice(ci * 128, (ci + 1) * 128)
                    tm = TM[ci]
                    beta = tm.ap[:, h:h + 1]
                    Eg = tm.ap[:, 48 + 32 + h:48 + 32 + h + 1]
                    KDc = tm.ap[:, 96 + 32 + h:96 + 32 + h + 1]
                    gc_ = tm.ap[:, 144 + 32 + h:144 + 32 + h + 1]
                    pt = PS.alloc()
                    PE_tr(pt.ap[:, 0:128], kc.ap[:, cs], ident, [kc], [pt])
                    PE_tr(pt.ap[:, 128:256], vc.ap[:, cs], ident, [vc], [pt])
                    kb = SP.alloc(); kbg = SP.alloc(); kd = SP.alloc(); bv = SP.alloc()
                    V("tensor_scalar", [pt, tm], [kb], out=kb.ap, in0=pt.ap[:, 0:128], scalar1=beta, scalar2=None, op0=ALU.mult)
                    V("tensor_scalar", [kb, tm], [kbg], out=kbg.ap, in0=kb.ap, scalar1=Eg, scalar2=None, op0=ALU.mult)
                    ACT(kd.ap, pt.ap[:, 0:128], AF.Identity, [pt, tm], [kd], scale=KDc)
                    ACT(bv.ap, pt.ap[:, 128:256], AF.Identity, [pt, tm], [bv], scale=beta)
                    PE_tr(pt.ap[:, 256:384], kb.ap, ident, [kb], [pt])
                    kbT = SP.alloc()
                    ACT(kbT.ap, pt.ap[:, 256:384], AF.Copy, [pt], [kbT])
                    PS.release(pt)
                    SP.release(kb)
                    yield
                    pr = PS.alloc()
                    PE_mm(pr.ap[:, 0:128], kc.ap[:, cs], kbT.ap, True, True, [kc, kbT], [pr])
                    PE_mm(pr.ap[:, 128:256], kbT.ap, kc.ap[:, cs], True, True, [kc, kbT], [pr])
                    PE_mm(pr.ap[:, 256:384], kc.ap[:, cs], qc.ap[:, cs], True, True, [kc, qc], [pr])
                    PE_mm(pr.ap[:, 384:512], sel(32 + h), GC.ap[:, cs], True, True, [GC, b_cst], [pr])
                    SP.release(kbT)
                    dts = SP.alloc(); dl = SP.alloc()
                    V("scalar_tensor_tensor", [pr, tm, b_cst], [dts], out=dts.ap, in0=pr.ap[:, 384:512], scalar=gc_, in1=NEGUS, op0=ALU.subtract, op1=ALU.add)
                    V("scalar_tensor_tensor", [pr, tm, b_cst], [dl], out=dl.ap, in0=pr.ap[:, 384:512], scalar=gc_, in1=POSLS, op0=ALU.subtract, op1=ALU.add)
                    ACT(dts.ap, dts.ap, AF.Exp, [dts], [dts])
                    ACT(dl.ap, dl.ap, AF.Exp, [dl], [dl], scale=-1.0)
                    R = RP.alloc()
                    V("scalar_tensor_tensor", [pr, dts], [R], out=R.ap[:, 0:128], in0=pr.ap[:, 0:128], scalar=-1.0, in1=dts.ap, op0=ALU.mult, op1=ALU.mult)
                    V("tensor_copy", [b_cst], [R], out=R.ap[:, 128:256], in_=ident)
                    A = SP.alloc()
                    V("scalar_tensor_tensor", [pr, dl], [A], out=A.ap, in0=pr.ap[:, 128:256], scalar=-1.0, in1=dl.ap, op0=ALU.mult, op1=ALU.mult)
                    SP.release(dl)
                    V("tensor_tensor", [dts, b_cst], [dts], out=dts.ap, in0=dts.ap, in1=ident, op=ALU.add)
                    qkT = dts
                    V("tensor_tensor", [pr, qkT], [qkT], out=qkT.ap, in0=pr.ap[:, 256:384], in1=qkT.ap, op=ALU.mult)
                    PS.release(pr)
                    yield
                    for j in range(7):
                        pn = PS.alloc()
                        if j < 6:
                            PE_mm(pn.ap[:, 0:256], A.ap, R.ap, True, True, [A, R], [pn])
                            PE_mm(pn.ap[:, 256:384], R.ap[:, 0:128], A.ap, True, True, [A, R], [pn])
                            R2 = RP.alloc(); A2 = SP.alloc()
                            ACT(R2.ap[:, 0:128], pn.ap[:, 0:128], AF.Copy, [pn], [R2])
                            V("tensor_tensor", [pn, R], [R2], out=R2.ap[:, 128:256], in0=pn.ap[:, 128:256], in1=R.ap[:, 128:256], op=ALU.add)
                            ACT(A2.ap, pn.ap[:, 256:384], AF.Copy, [pn], [A2])
                            RP.release(R); SP.release(A)
                            R, A = R2, A2
                        else:
                            PE_mm(pn.ap[:, 0:128], A.ap, R.ap[:, 128:256], True, True, [A, R], [pn])
                            V("tensor_tensor", [pn, R], [R], out=R.ap[:, 128:256], in0=pn.ap[:, 0:128], in1=R.ap[:, 128:256], op=ALU.add)
                            SP.release(A)
                        PS.release(pn)
                        yield
                    Q = R.ap[:, 128:256]
                    pw = PS.alloc()
                    PE_mm(pw.ap[:, 0:128], kbg.ap, Q, True, True, [kbg, R], [pw])
                    nwT = SP.alloc()
                    ACT(nwT.ap, pw.ap[:, 0:128], AF.Copy, [pw], [nwT], scale=-1.0)
                    SP.release(kbg)
                    yield
                    Sh = Sg[:, hs, :]
                    PE_mm(pw.ap[:, 128:256], Q, bv.ap, True, False, [R, bv], [pw])
                    PE_mm(pw.ap[:, 128:256], nwT.ap, Sh, False, True, [nwT, b_Sg[hs]], [pw])
                    vn = SP.alloc()
                    ACT(vn.ap, pw.ap[:, 128:256], AF.Copy, [pw], [vn])
                    RP.release(R); SP.release(bv); SP.release(nwT)
                    yield
                    PE_mm(pw.ap[:, 256:384], Sh, qd.ap[:, cs], True, False, [b_Sg[hs], qd], [pw])
                    PE_mm(pw.ap[:, 256:384], vn.ap, qkT.ap, False, True, [vn, qkT], [pw])
                    PE_mm(pw.ap[:, 384:512], kd.ap, vn.ap, True, True, [kd, vn], [pw])
                    ACT(oT.ap[:, cs], pw.ap[:, 256:384], AF.Copy, [pw], [oT])
                    V("scalar_tensor_tensor", [b_Sg[hs], gts, pw], [b_Sg[hs]], out=Sh, in0=Sh, scalar=gts.ap[:, ci:ci + 1], in1=pw.ap[:, 384:512],
                      op0=ALU.mult, op1=ALU.add)
                    PS.release(pw)
                    SP.release(vn); SP.release(kd); SP.release(qkT)
                    yield
                SP.release(gts)
                BP.release(qd); BP.release(qc); BP.release(kc); BP.release(vc)
                pmz = inproj(l, ub + 3 * ust)
                sz = BP.alloc()
                ACT(sz.ap, pmz.ap, AF.Silu, [pmz], [sz])
                PS.release(pmz)
                yield
                r = rstd_bc(1.0 / 128, EPS, [oT.ap], [oT.buf])
                V("tensor_tensor", [oT, r], [oT], out=oT.ap, in0=oT.ap, in1=r.ap, op=ALU.mult)
                BP.release(r)
                V("scalar_tensor_tensor", [oT, b_pv, sz], [b_yT[h]], out=yT[:, h, :], in0=oT.ap, scalar=pvc(l, PV_GNW), in1=sz.ap, op0=ALU.mult, op1=ALU.mult)
                BP.release(oT); BP.release(sz)
                yield

            GDN_PHASE_MARK = None

            if stop_after <= 3:
                raise _Stop()
            def ssd_gen():
                sx = [BP.alloc() for _ in range(4)]
                sbm = [BP.alloc() for _ in range(2)]
                scm = [BP.alloc() for _ in range(2)]
                for j in range(4):
                    pm = inproj(l, 17 + j)
                    conv(l, pm, 12 + j, pv[:, l, PV_SCW + 4 * j:PV_SCW + 4 * j + 4], pvc(l, PV_SCB + j), AF.Silu, sx[j])
                    yield
                for g in range(2):
                    pm = inproj(l, 21 + g)
                    conv(l, pm, 16 + g, pv[:, l, PV_SCW + 4 * (4 + g):PV_SCW + 4 * (4 + g) + 4], pvc(l, PV_SCB + 4 + g), AF.Silu, sbm[g])
                    yield
                for g in range(2):
                    pm = inproj(l, 23 + g)
                    conv(l, pm, 18 + g, pv[:, l, PV_SCW + 4 * (6 + g):PV_SCW + 4 * (6 + g) + 4], pvc(l, PV_SCB + 6 + g), AF.Silu, scm[g])
                    yield
                yb = [BP.alloc() for _ in range(4)]
                Sst = Ss[:, l, :]

                def chunk_pre(ci, res):
                    cs = slice(ci * 128, (ci + 1) * 128)
                    tm = TM[ci]
                    dt8 = tm.ap[:, 48 + 40:48 + 48]
                    kd8 = tm.ap[:, 96 + 40:96 + 48]
                    pt = PS.alloc()
                    for j in range(4):
                        PE_tr(pt.ap[:, j * 128:(j + 1) * 128], sx[j].ap[:, cs], ident, [sx[j]], [pt])
                    xdt = BP.alloc(); xdd = BP.alloc()
                    V("tensor_tensor", [pt, tm], [xdt], out=xdt.ap.rearrange("p (h q) -> p h q", h=8), in0=pt.ap.rearrange("p (h q) -> p h q", h=8),
                      in1=dt8.unsqueeze(2).to_broadcast([128, 8, 64]), op=ALU.mult)
                    PS.release(pt)
                    V("tensor_tensor", [xdt, tm], [xdd], out=xdd.ap.rearrange("p (h q) -> p h q", h=8), in0=xdt.ap.rearrange("p (h q) -> p h q", h=8),
                      in1=kd8.unsqueeze(2).to_broadcast([128, 8, 64]), op=ALU.mult)
                    yield
                    pt = PS.alloc()
                    for g in range(2):
                        PE_tr(pt.ap[:, g * 128:(g + 1) * 128], sbm[g].ap[:, cs], ident, [sbm[g]], [pt])
                        PE_mm(pt.ap[:, 256 + g * 128:256 + (g + 1) * 128], sbm[g].ap[:, cs], scm[g].ap[:, cs], True, True, [sbm[g], scm[g]], [pt])
                    bmt = RP.alloc()
                    ACT(bmt.ap, pt.ap[:, 0:256], AF.Copy, [pt], [bmt])
                    cbm = RP.alloc()
                    ACT(cbm.ap, pt.ap[:, 256:512], AF.Copy, [pt], [cbm])
                    PS.release(pt)
                    yield
                    Wl = [BP.alloc(), BP.alloc()]
                    Eb = [BP.alloc(), BP.alloc()]
                    cdec = SP.alloc()
                    for b2 in range(2):
                        pa = PS.alloc()
                        for h4 in range(4):
                            PE_mm(pa.ap[:, h4 * 128:(h4 + 1) * 128], sel(40 + b2 * 4 + h4), GC.ap[:, cs], True, True, [GC, b_cst], [pa])
                        for h4 in range(4):
                            hh, o = b2 * 4 + h4, h4 * 128
                            V("scalar_tensor_tensor", [pa, tm, b_cst], [Wl[b2]], out=Wl[b2].ap[:, o:o + 128], in0=pa.ap[:, o:o + 128],
                              scalar=tm.ap[:, 144 + 40 + hh:144 + 40 + hh + 1], in1=NEGUI, op0=ALU.subtract, op1=ALU.add)
                        ACT(Eb[b2].ap, pa.ap, AF.Exp, [pa], [Eb[b2]])
                        PS.release(pa)
                        ACT(Wl[b2].ap, Wl[b2].ap, AF.Exp, [Wl[b2]], [Wl[b2]])
                        yield
                        V("tensor_tensor", [Wl[b2], cbm], [Wl[b2]], out=Wl[b2].ap.rearrange("p (h c) -> p h c", h=4),
                          in0=Wl[b2].ap.rearrange("p (h c) -> p h c", h=4),
                          in1=cbm.ap[:, b2 * 128:(b2 + 1) * 128].unsqueeze(1).to_broadcast([128, 4, 128]), op=ALU.mult)
                        V("tensor_copy", [Eb[b2]], [cdec], out=cdec.ap[:, b2 * 4:(b2 + 1) * 4],
                          in_=Eb[b2].ap.rearrange("p (h c) -> p h c", h=4)[:, :, 127])
                        V("tensor_tensor", [Eb[b2], scm[b2]], [Eb[b2]], out=Eb[b2].ap.rearrange("p (h c) -> p h c", h=4),
                          in0=Eb[b2].ap.rearrange("p (h c) -> p h c", h=4),
                          in1=scm[b2].ap[:, cs].unsqueeze(1).to_broadcast([128, 4, 128]), op=ALU.mult)
                        yield
                    RP.release(cbm)
                    res[ci] = (xdt, xdd, bmt, Wl, Eb, cdec)

                def chunk_fin(ci, res):
                    cs = slice(ci * 128, (ci + 1) * 128)
                    xdt, xdd, bmt, Wl, Eb, cdec = res[ci]
                    py = PS.alloc()
                    for hh in range(8):
                        b2, o = hh // 4, (hh % 4) * 128
                        blk, half = hh // 2, hh % 2
                        outp = py.ap[half * 64:(half + 1) * 64, blk * 128:(blk + 1) * 128]
                        PE_mm(outp, xdt.ap[:, hh * 64:(hh + 1) * 64], Wl[b2].ap[:, o:o + 128], True, False, [xdt, Wl[b2]], [py])
                        PE_mm(outp, Sst[:, hh * 64:(hh + 1) * 64], Eb[b2].ap[:, o:o + 128], False, True, [b_Ss[l], Eb[b2]], [py])
                    pu = PS.alloc()
                    for g in range(2):
                        PE_mm(pu.ap[:, g * 256:(g + 1) * 256], bmt.ap[:, g * 128:(g + 1) * 128], xdd.ap[:, g * 256:(g + 1) * 256], True, True, [bmt, xdd], [pu])
                    for blk in range(4):
                        V("scalar_tensor_tensor", [sx[blk], b_pv, py], [yb[blk]], out=yb[blk].ap[:, cs], in0=sx[blk].ap[:, cs],
                          scalar=pvc(l, PV_SD + blk), in1=py.ap[:, blk * 128:(blk + 1) * 128], op0=ALU.mult, op1=ALU.add)
                    PS.release(py)
                    V("tensor_tensor", [b_Ss[l], cdec], [b_Ss[l]], out=Sst.rearrange("p (h q) -> p h q", h=8), in0=Sst.rearrange("p (h q) -> p h q", h=8),
                      in1=cdec.ap[:, 0:8].unsqueeze(2).to_broadcast([128, 8, 64]), op=ALU.mult)
                    V("tensor_tensor", [b_Ss[l], pu], [b_Ss[l]], out=Sst, in0=Sst, in1=pu.ap, op=ALU.add)
                    PS.release(pu)
                    BP.release(Wl[0]); BP.release(Wl[1]); BP.release(Eb[0]); BP.release(Eb[1]); BP.release(xdt)
                    RP.release(bmt); SP.release(cdec); BP.release(xdd)

                res = {}
                for pair in ((0, 1), (2, 3)):
                    gens = [chunk_pre(ci, res) for ci in pair]
                    while gens:
                        for g_ in list(gens):
                            try:
                                next(g_)
                            except StopIteration:
                                gens.remove(g_)
                        yield
                    for ci in pair:
                        chunk_fin(ci, res)
                        yield
                for s_ in sx + sbm + scm:
                    BP.release(s_)
                BP.release(GC); BP.release(EE)
                for tm in TM:
                    RP.release(tm)
                for j in range(4):
                    pm = inproj(l, 25 + j)
                    sz = BP.alloc()
                    ACT(sz.ap, pm.ap, AF.Silu, [pm], [sz])
                    PS.release(pm)
                    V("tensor_tensor", [yb[j], sz], [yb[j]], out=yb[j].ap, in0=yb[j].ap, in1=sz.ap, op=ALU.mult)
                    BP.release(sz)
                    yield
                for g in range(2):
                    r = rstd_bc(1.0 / 256, EPS, [yb[2 * g].ap, yb[2 * g + 1].ap], [yb[2 * g].buf, yb[2 * g + 1].buf])
                    for j in (2 * g, 2 * g + 1):
                        V("scalar_tensor_tensor", [yb[j], b_pv, r], [b_yT[4 + j]], out=yT[:, 4 + j, :], in0=yb[j].ap, scalar=pvc(l, PV_SNW + j), in1=r.ap,
                          op0=ALU.mult, op1=ALU.mult)
                    BP.release(r)
                    yield
                for s_ in yb:
                    BP.release(s_)


            def lru_gen():
                wl_ap, wl_b = W.get(l, 37)
                S.op("pool", lambda e: e.tensor_copy(out=lbdt[:], in_=wl_ap), [wl_b], [b_lbd])
                lbd_ap, lbd_b = lbdt[:], b_lbd
                yield
                for j in range(4):
                    xc = BP.alloc()
                    pm = inproj(l, 29 + 2 * j)
                    conv(l, pm, 20 + j, pv[:, l, PV_LCW + 4 * j:PV_LCW + 4 * j + 4], pvc(l, PV_LCB + j), AF.Identity, xc)
                    V("tensor_copy", [xc], [b_lxb[j]], out=lxb[:, j, :], in_=xc.ap)
                    yield
                    pg = inproj(l, 30 + 2 * j)
                    gg = BP.alloc()
                    ACT(gg.ap, pg.ap, AF.Gelu_apprx_tanh, [pg], [gg])
                    PS.release(pg)
                    yield
                    pa_ = PS.alloc(); pi_ = PS.alloc()
                    PE_mm(pa_.ap, lbd_ap[:, j * 128:(j + 1) * 128], lxb[:, j, :], True, True, [lbd_b, b_lxb[j]], [pa_])
                    PE_mm(pi_.ap, lbd_ap[:, (4 + j) * 128:(5 + j) * 128], lxb[:, j, :], True, True, [lbd_b, b_lxb[j]], [pi_])
                    ra = BP.alloc(); ri = BP.alloc()
                    ACT(ra.ap, pa_.ap, AF.Sigmoid, [pa_, b_pv], [ra], bias=pvc(l, PV_LBA + j))
                    ACT(ri.ap, pi_.ap, AF.Sigmoid, [pi_, b_pv], [ri], bias=pvc(l, PV_LBX + j))
                    PS.release(pa_); PS.release(pi_)
                    yield
                    ACT(ra.ap, ra.ap, AF.Exp, [ra, b_mod], [ra], scale=m[:, 52 + j:53 + j])
                    mu = BP.alloc()
                    V("tensor_tensor", [ra], [mu], out=mu.ap, in0=ra.ap, in1=ra.ap, op=ALU.mult)
                    ACT(mu.ap, mu.ap, AF.Sqrt, [mu], [mu], scale=-1.0, bias=1.0)
                    V("tensor_tensor", [ri, xc], [ri], out=ri.ap, in0=ri.ap, in1=xc.ap, op=ALU.mult)
                    yield
                    V("tensor_tensor", [ri, mu], [mu], out=mu.ap, in0=ri.ap, in1=mu.ap, op=ALU.mult)
                    hb = b_hl[l * 4 + j]
                    V("tensor_tensor_scan", [ra, mu, hb], [xc], out=xc.ap, data0=ra.ap, data1=mu.ap, initial=hl[:, l * 4 + j, 1:2], op0=ALU.mult, op1=ALU.add)
                    V("tensor_copy", [xc], [hb], out=hl[:, l * 4 + j, :], in_=xc.ap[:, TT - 2:TT])
                    V("tensor_tensor", [xc, gg], [b_yT[8 + j]], out=yT[:, 8 + j, :], in0=xc.ap, in1=gg.ap, op=ALU.mult)
                    BP.release(ra); BP.release(ri); BP.release(mu); BP.release(xc); BP.release(gg)
                    yield

            interleave([small_gen(), gdn_head(0), gdn_head(1), gdn_head(2), gdn_head(3), lru_gen()])
            interleave([ssd_gen()])

            if dbg and not dbg_state["done"]:
                dbg_state["done"] = True
                for k in range(12):
                    t = BP.alloc()
                    V("tensor_copy", [b_yT[k]], [t], out=t.ap, in_=yT[:, k, :])
                    DMA("sp", dbg_d[:, k, :], t.ap, [t], [])
                    BP.release(t)

            if stop_after <= 5:
                raise _Stop()
            for ob in range(8):
                acc = BP.alloc()
                for r in range(3):
                    u = 38 + (ob * 3 + r) * 2
                    pb = proj(l, u, lambda c, r=r: yT[:, 4 * r + c, :], b_yT[4 * r:4 * r + 4], nk=4)
                    pg = inproj(l, u + 1)
                    sg = BP.alloc()
                    ACT(sg.ap, pg.ap, AF.Sigmoid, [pg], [sg])
                    PS.release(pg)
                    if r == 0:
                        V("tensor_tensor", [pb, sg], [acc], out=acc.ap, in0=pb.ap, in1=sg.ap, op=ALU.mult)
                    else:
                        V("tensor_tensor", [pb, sg], [sg], out=sg.ap, in0=pb.ap, in1=sg.ap, op=ALU.mult)
                        if r == 1:
                            V("tensor_tensor", [acc, sg], [acc], out=acc.ap, in0=acc.ap, in1=sg.ap, op=ALU.add)
                        else:
                            V("tensor_tensor", [acc, sg], [b_mT[ob]], out=mT[:, ob, :], in0=acc.ap, in1=sg.ap, op=ALU.add)
                    PS.release(pb)
                    BP.release(sg)
                BP.release(acc)
            G1 = MOD(l, 2)
            for ob in range(8):
                pm = proj(l, 86 + ob, lambda c: mT[:, c, :], b_mT)
                V("scalar_tensor_tensor", [pm, b_mod, b_xT[ob]], [b_xT[ob]], out=xT[:, ob, :], in0=pm.ap, scalar=G1[:, ob:ob + 1], in1=xT[:, ob, :],
                  op0=ALU.mult, op1=ALU.add)
                PS.release(pm)
                stats_acc(ob)

            if stop_after <= 6:
                raise _Stop()
            norm_to_hT(l, 3, 4)
            asl = [BP.alloc() for _ in range(16)]
            aT = [s_.ap.bitcast(BF16).rearrange("p (a t) -> p a t", a=2) for s_ in asl]

            def aT_ap(c):
                return aT[c // 2][:, c % 2, :]

            for ob in range(32):
                pm = inproj(l, 94 + ob)
                sq = BP.alloc()
                ACT(sq.ap, pm.ap, AF.Relu, [pm], [sq])
                V("tensor_tensor", [sq, pm], [asl[ob // 2]], out=aT_ap(ob), in0=sq.ap, in1=pm.ap, op=ALU.mult)
                PS.release(pm)
                BP.release(sq)
            G2 = MOD(l, 5)
            for ob in range(8):
                pm = PS.alloc()
                for qtr in range(4):
                    wap, wb = W.get(l, 126 + ob * 4 + qtr)
                    for cc in range(8):
                        c = qtr * 8 + cc
                        PE_mm(pm.ap, wap[:, cc * 128:(cc + 1) * 128], aT_ap(c), c == 0, c == 31, [wb, asl[c // 2]], [pm])
                V("scalar_tensor_tensor", [pm, b_mod, b_xT[ob]], [b_xT[ob]], out=xT[:, ob, :], in0=pm.ap, scalar=G2[:, ob:ob + 1], in1=xT[:, ob, :],
                  op0=ALU.mult, op1=ALU.add)
                PS.release(pm)
                stats_acc(ob)
            for s_ in asl:
                BP.release(s_)

        for ti in range(NT):
            xs = [BP.alloc() for _ in range(8)]
            for s_ in range(4):
                for hf in range(2):
                    DMA("sp", xs[2 * s_ + hf].ap, x_d[ti * TT + s_ * 128:ti * TT + (s_ + 1) * 128, hf * 512:(hf + 1) * 512], [], [xs[2 * s_ + hf]])
            for c in range(8):
                pt = PS.alloc()
                for s_ in range(4):
                    src = xs[2 * s_ + c // 4]
                    PE_tr(pt.ap[:, s_ * 128:(s_ + 1) * 128], src.ap[:, (c % 4) * 128:(c % 4 + 1) * 128], ident, [src], [pt])
                ACT(xT[:, c, :], pt.ap, AF.Copy, [pt], [b_xT[c]])
                PS.release(pt)
                stats_acc(c)
            for s_ in xs:
                BP.release(s_)
            try:
                for l in range(L):
                    layer(l)
            except _Stop:
                for P_ in (BP, SP, RP, RAWP, PS):
                    P_.free = list(range(len(P_.bufs)))
                xstat["pm"] = None
                for c in range(8):
                    stats_acc(c)
            r = x_rstd()
            xo = [BP.alloc() for _ in range(8)]
            fn = [BP.alloc() for _ in range(8)]
            for c in range(8):
                V("scalar_tensor_tensor", [b_xT[c], b_pv, r], [fn[c]], out=fn[c].ap, in0=xT[:, c, :], scalar=pv[:, 0, PV_FNW + c:PV_FNW + c + 1], in1=r.ap,
                  op0=ALU.mult, op1=ALU.mult)
            BP.release(r)
            for s_ in range(4):
                for hf in range(2):
                    pt = PS.alloc()
                    for cc in range(4):
                        c = hf * 4 + cc
                        PE_tr(pt.ap[:, cc * 128:(cc + 1) * 128], fn[c].ap[:, s_ * 128:(s_ + 1) * 128], ident, [fn[c]], [pt])
                    dst = xo[2 * s_ + hf]
                    ACT(dst.ap, pt.ap, AF.Copy, [pt], [dst])
                    PS.release(pt)
                    DMA("sp", y_d[ti * TT + s_ * 128:ti * TT + (s_ + 1) * 128, hf * 512:(hf + 1) * 512], dst.ap, [dst], [])
            for s_ in xo + fn:
                BP.release(s_)
        S.final_wait("sp")
        build_program.recorded = W.recorded
        build_program.stats = dict(n_ops=S.n_ops, per_eng={e: len(S.ops[e]) for e in ENGS},
                                   minfree=dict(big=BP.minfree, sm=SP.minfree, rp=RP.minfree))
        S.emit()
    return nc


def _unit(Wm, cols, nk=8):
    K = Wm.shape[0]
    out = np.zeros((128, nk, 128), np.float32)
    cols = np.asarray(cols)
    ok = cols >= 0
    sub = Wm[:nk * 128][:, cols[ok]]
    out[:, :, ok] = sub.reshape(nk, 128, -1).transpose(1, 0, 2)
    return out.reshape(128, nk * 128)


def _colvec(v):
    v = np.asarray(v, np.float32)
    return np.ascontiguousarray(v.reshape(-1, 128).T)


def make_consts():
    c = np.zeros((128, NCST), np.float32)
    p = np.arange(128)[:, None]
    f = np.arange(128)[None, :]
    c[:, C_ID:C_ID + 128] = (p == f)
    c[:, C_ONES:C_ONES + 128] = 1.0
    c[:, C_NEGUS:C_NEGUS + 128] = np.where(f > p, 0.0, -1e30)
    c[:, C_NEGUI:C_NEGUI + 128] = np.where(f >= p, 0.0, -1e30)
    c[:, C_POSLS:C_POSLS + 128] = np.where(f < p, 0.0, 1e30)
    for k, r in enumerate(SELROWS):
        c[r, C_SEL + k * 128:C_SEL + (k + 1) * 128] = 1.0
    return c


def pack_layer_weights(w_in, w_branch, w_out, w_up, w_down, lru_w_a, lru_w_x):
    units = np.zeros((NU, 128, 1024), np.float32)
    ar = np.arange(128)
    small = -np.ones(128, np.int64)
    small[0:4] = 2048 + np.arange(4)
    small[32:36] = 2052 + np.arange(4)
    small[40:48] = 3592 + np.arange(8)
    units[0] = _unit(w_in, small)
    for h in range(4):
        ub, ust = 1 + 4 * h, 1
        units[ub] = _unit(w_in, 0 + 128 * h + ar)
        units[ub + ust] = _unit(w_in, 512 + 128 * h + ar)
        units[ub + 2 * ust] = _unit(w_in, 1024 + 128 * h + ar)
        units[ub + 3 * ust] = _unit(w_in, 1536 + 128 * h + ar)
    for j in range(4):
        units[17 + j] = _unit(w_in, 2056 + 128 * j + ar)
        units[25 + j] = _unit(w_in, 2568 + 128 * j + ar)
        units[29 + 2 * j] = _unit(w_in, 3600 + 128 * j + ar)
        units[30 + 2 * j] = _unit(w_in, 4112 + 128 * j + ar)
    for g in range(2):
        units[21 + g] = _unit(w_in, 3080 + 128 * g + ar)
        units[23 + g] = _unit(w_in, 3336 + 128 * g + ar)
    bd = np.zeros((128, 8, 128), np.float32)
    for j in range(4):
        for hb in range(2):
            blk = 2 * j + hb
            bd[hb * 64:(hb + 1) * 64, j, hb * 64:(hb + 1) * 64] = lru_w_a[blk]
            bd[hb * 64:(hb + 1) * 64, 4 + j, hb * 64:(hb + 1) * 64] = lru_w_x[blk]
    units[37] = bd.reshape(128, 1024)
    for ob in range(8):
        for r in range(3):
            u = 38 + (ob * 3 + r) * 2
            units[u, :, 0:512] = _unit(w_branch[r], ob * 128 + ar, nk=4)
            units[u + 1] = _unit(w_in, 4624 + r * 1024 + ob * 128 + ar)
        units[86 + ob] = _unit(w_out, ob * 128 + ar)
    for ob in range(32):
        units[94 + ob] = _unit(w_up, ob * 128 + ar)
    for ob in range(8):
        for qtr in range(4):
            units[126 + ob * 4 + qtr] = _unit(w_down[qtr * 1024:(qtr + 1) * 1024], ob * 128 + ar)
    return units


def pack_ada(ada_w_l):
    out = np.zeros((N_ADA, 128, 512), np.float32)
    ar = np.arange(128)
    for ob in range(48):
        for half in range(2):
            out[ob * 2 + half] = _unit(ada_w_l[half * 512:(half + 1) * 512], ob * 128 + ar, nk=4)
    return out


def pack_pv(l, p):
    v = np.zeros((128, NPV), np.float32)
    v[:, PV_NM:PV_NM + 8] = _colvec(p["norm_mix"][l])
    v[:, PV_NMLP:PV_NMLP + 8] = _colvec(p["norm_mlp"][l])
    v[:, PV_ADAB:PV_ADAB + 48] = _colvec(p["ada_b"][l])
    gcw = p["gdn_conv_w"][l]
    for b in range(12):
        v[:, PV_GCW + 4 * b:PV_GCW + 4 * b + 4] = gcw[:, b * 128:(b + 1) * 128].T
    scw = p["ssd_conv_w"][l]
    for b in range(8):
        v[:, PV_SCW + 4 * b:PV_SCW + 4 * b + 4] = scw[:, b * 128:(b + 1) * 128].T
    v[:, PV_SCB:PV_SCB + 8] = _colvec(p["ssd_conv_b"][l])
    lcw = p["lru_conv_w"][l]
    for b in range(4):
        v[:, PV_LCW + 4 * b:PV_LCW + 4 * b + 4] = lcw[:, b * 128:(b + 1) * 128].T
    v[:, PV_LCB:PV_LCB + 4] = _colvec(p["lru_conv_b"][l])
    v[:, PV_LBA:PV_LBA + 4] = _colvec(p["lru_b_a"][l])
    v[:, PV_LBX:PV_LBX + 4] = _colvec(p["lru_b_x"][l])
    v[:, PV_LAM:PV_LAM + 4] = _colvec(p["lru_lambda"][l])
    v[:, PV_GNW] = p["gdn_norm"][l]
    v[:, PV_SNW:PV_SNW + 4] = _colvec(p["ssd_norm"][l])
    v[:, PV_SD:PV_SD + 4] = _colvec(np.repeat(np.asarray(p["ssd_d"][l]), 64))
    v[32:36, PV_RB] = p["gdn_dt_bias"][l]
    v[40:48, PV_RB] = p["ssd_dt_bias"][l]
    v[32:36, PV_RA] = p["gdn_a_log"][l]
    v[40:48, PV_RA] = p["ssd_a_log"][l]
    v[:, PV_FNW:PV_FNW + 8] = _colvec(p["final_norm"])
    return v


def prepare_shared(p, L):
    p = {k: np.asarray(v, np.float32) for k, v in p.items()}
    wpack = np.concatenate([pack_layer_weights(p["w_in"][l], p["w_branch"][l], p["w_out"][l], p["w_up"][l], p["w_down"][l],
                                               p["lru_w_a"][l], p["lru_w_x"][l]) for l in range(L)], axis=0)
    wada = np.concatenate([pack_ada(p["ada_w"][l]) for l in range(L)], axis=0)
    pvv = np.stack([pack_pv(l, p) for l in range(L)], axis=0)
    return {"wpack": wpack, "wada": wada, "pv": pvv, "cst": make_consts()}


_ORDER = []


def unit_order():
    if not _ORDER:
        build_program(TT, 1, order=None)
        rec = list(build_program.recorded)
        assert sorted(rec) == list(range(NU)), len(rec)
        _ORDER.extend(rec)
    return list(_ORDER)


def kernel(**inputs):
    x = np.asarray(inputs["x"], np.float32)
    c = np.asarray(inputs["c"], np.float32)
    B, T, _ = x.shape
    L = inputs["w_in"].shape[0]
    params = {k: v for k, v in inputs.items() if k not in ("x", "c")}
    shared = prepare_shared(params, L)
    nc = build_program(T, L, order=unit_order())
    in_maps = []
    for b in range(B):
        m = dict(shared)
        m["x"] = np.ascontiguousarray(x[b])
        m["cT"] = _colvec(c[b])
        in_maps.append(m)
    res = run_bass_kernel_spmd(nc, in_maps, core_ids=list(range(B)))
    return np.stack([np.asarray(r["y"], np.float32) for r in res.results], axis=0)
```

```python
import contextlib
import os as _os
import numpy as np
import concourse.bass as bass
import concourse.mybir as mybir
from concourse.bass_utils import run_bass_kernel_spmd

F32 = mybir.dt.float32
BF16 = mybir.dt.bfloat16
AF = mybir.ActivationFunctionType
ALU = mybir.AluOpType

D = 1024
TT = 512
NU = 158
N_ADA = 96
EPS = 1e-6
GDN_IL = 2
NPV = 208
NCST = 5 * 128 + 12 * 128
PV_NM, PV_NMLP, PV_ADAB, PV_GCW, PV_SCW, PV_SCB, PV_LCW, PV_LCB = 0, 8, 16, 64, 112, 144, 152, 168
PV_LBA, PV_LBX, PV_LAM, PV_GNW, PV_SNW, PV_SD, PV_RB, PV_RA, PV_FNW = 172, 176, 180, 184, 185, 189, 193, 194, 195
C_ID, C_ONES, C_NEGUS, C_NEGUI, C_POSLS, C_SEL = 0, 128, 256, 384, 512, 640
SELROWS = [32, 33, 34, 35, 40, 41, 42, 43, 44, 45, 46, 47]

ENGS = ("pe", "act", "dve", "pool", "sp")
N_DMA_SEMS = 24


class Buf:
    __slots__ = ("name", "w", "r", "excl")

    def __init__(self, name="", excl=False):
        self.name = name
        self.w = None
        self.r = []
        self.excl = excl


class Sched:
    def __init__(self, nc):
        self.nc = nc
        self.ops = {e: [] for e in ENGS}
        self.cnt = {e: 0 for e in ENGS}
        self.waited = {e: {} for e in ENGS}
        self.dma_i = 0
        self.dma_cnt = [0] * N_DMA_SEMS
        self.sw_gen = {}
        self.n_ops = 0

    def _need(self, eng, tok, waits, is_raw):
        if tok is None:
            return
        semkey, val, teng = tok
        if teng == eng and semkey == eng and not is_raw and eng == "pe":
            return
        if self.waited[eng].get(semkey, 0) >= val:
            return
        if waits.get(semkey, 0) < val:
            waits[semkey] = val

    def op(self, eng, fn, reads=(), writes=(), dma=False, swslot=None, noinc=False):
        waits = {}
        for b in reads:
            self._need(eng, b.w, waits, True)
            if b.excl:
                for t in b.r:
                    if t[2] != eng:
                        self._need(eng, t, waits, False)
        for b in writes:
            self._need(eng, b.w, waits, False)
            for t in b.r:
                self._need(eng, t, waits, False)
        clear = None
        if swslot is not None:
            gen = self.sw_gen.get(swslot, 0)
            self.sw_gen[swslot] = gen + 1
            tok = ("sw%d:%d" % (swslot, gen), 16, "dma")
            inc = ("sw%d" % swslot, 16)
            if gen > 0:
                clear = "sw%d" % swslot
        elif dma:
            k = self.dma_i % N_DMA_SEMS
            self.dma_i += 1
            if self.dma_cnt[k] > 0:
                self._need(eng, ("dma%d" % k, 16 * self.dma_cnt[k], "dma"), waits, False)
            self.dma_cnt[k] += 1
            tok = ("dma%d" % k, 16 * self.dma_cnt[k], "dma")
            inc = ("dma%d" % k, 16)
        elif noinc:
            tok = (eng, self.cnt[eng] + 1, eng)
            inc = None
        else:
            self.cnt[eng] += 1
            tok = (eng, self.cnt[eng], eng)
            inc = (eng, 1)
        for sk, v in waits.items():
            self.waited[eng][sk] = v
        self.ops[eng].append((fn, tuple(waits.items()), inc, clear))
        for b in reads:
            b.r.append(tok)
            if len(b.r) > 64:
                b.r = _compact(b.r)
        for b in writes:
            b.w = tok
            b.r = []
        self.n_ops += 1
        return tok

    def final_wait(self, eng="sp"):
        waits = {}
        for e in ENGS:
            if self.cnt[e] and e != eng:
                self._need(eng, (e, self.cnt[e], e), waits, False)
        for k in range(N_DMA_SEMS):
            if self.dma_cnt[k]:
                self._need(eng, ("dma%d" % k, 16 * self.dma_cnt[k], "dma"), waits, False)
        self.ops[eng].append((None, tuple(waits.items()), None, None))

    def emit(self):
        nc = self.nc
        with contextlib.ExitStack() as st:
            sems = {}
            for e in ENGS:
                sems[e] = st.enter_context(nc.semaphore("s_" + e))
            for k in range(N_DMA_SEMS):
                sems["dma%d" % k] = st.enter_context(nc.semaphore("s_dma%d" % k))
            for k in self.sw_gen:
                sems["sw%d" % k] = st.enter_context(nc.semaphore("s_sw%d" % k))
            block = st.enter_context(nc.Block())

            def run(eng_name):
                def body(eng):
                    for fn, waits, inc, clear in self.ops[eng_name]:
                        for sk, v in waits:
                            eng.wait_ge(sems[sk.split(":")[0]], v)
                        if fn is None:
                            continue
                        if clear is not None:
                            eng.wait_ge(sems[clear], 16)
                            eng.sem_clear(sems[clear])
                        ins = fn(eng)
                        if inc is not None:
                            ins.then_inc(sems[inc[0]], inc[1])
                return body

            block.tensor(run("pe"))
            block.scalar(run("act"))
            block.vector(run("dve"))
            block.gpsimd(run("pool"))
            block.sync(run("sp"))


def _compact(toks):
    best = {}
    for t in toks:
        if t[0] not in best or best[t[0]][1] < t[1]:
            best[t[0]] = t
    return list(best.values())


class Slot:
    __slots__ = ("pool", "i", "ap", "buf")


class SlotPool:
    def __init__(self, tile, n, name):
        self.tile = tile
        self.n = n
        self.free = list(range(n))
        self.bufs = [Buf("%s%d" % (name, i)) for i in range(n)]
        self.name = name
        self.minfree = n

    def alloc(self, idx=None):
        if not self.free:
            raise RuntimeError("pool %s exhausted" % self.name)
        if idx is None:
            i = self.free.pop(0)
        else:
            self.free.remove(idx)
            i = idx
        self.minfree = min(self.minfree, len(self.free))
        s = Slot()
        s.pool, s.i, s.ap, s.buf = self, i, self.tile[:, i, :], self.bufs[i]
        return s

    def release(self, s):
        assert s.i not in self.free
        self.free.append(s.i)


class _Stop(Exception):
    pass


def build_program(T, L, dbg=False, stop_after=99, order=None):
    assert T % TT == 0
    NT = T // TT
    nc = bass.Bass("TRN2", target_bir_lowering=False)
    x_d = nc.dram_tensor("x", [T, D], F32, kind="ExternalInput").ap()
    cT_d = nc.dram_tensor("cT", [128, 8], F32, kind="ExternalInput").ap()
    pv_d = nc.dram_tensor("pv", [L, 128, NPV], F32, kind="ExternalInput").ap()
    cst_d = nc.dram_tensor("cst", [128, NCST], F32, kind="ExternalInput").ap()
    wada_d = nc.dram_tensor("wada", [L * N_ADA, 128, 512], F32, kind="ExternalInput").ap()
    wp_d = nc.dram_tensor("wpack", [L * NU, 128, 1024], F32, kind="ExternalInput").ap()
    y_d = nc.dram_tensor("y", [T, D], F32, kind="ExternalOutput").ap()
    wbf_d = nc.dram_tensor("wbf", [L * NU, 128, 1024], BF16, kind="Internal").ap()
    b_wbf = [Buf("wbf%d" % i) for i in range(L * NU)]
    dbg_d = None
    if dbg:
        dbg_d = nc.dram_tensor("dbg", [128, 12, TT], F32, kind="ExternalOutput").ap()

    st = contextlib.ExitStack()
    with st:
        S = Sched(nc)

        def sbt(name, shape, dt=F32):
            return st.enter_context(nc.sbuf_tensor("sb_" + name, shape, dt))

        NBIG, NSM, NRP, NWS, NRAW, NST = 33, 40, 14, 6, 3, 3
        xT = sbt("xT", [128, 8, TT]); b_xT = [Buf("xT%d" % c) for c in range(8)]
        hT = sbt("hT", [128, 8, TT], BF16); b_hT = [Buf("hT%d" % c) for c in range(8)]
        yT = sbt("yT", [128, 12, TT], BF16); b_yT = [Buf("yT%d" % c) for c in range(12)]
        mT = sbt("mT", [128, 8, TT], BF16); b_mT = [Buf("mT%d" % c) for c in range(8)]
        wring = sbt("wring", [128, NWS, 1024], BF16); b_wr = [Buf("wr%d" % i) for i in range(NWS)]
        wstage = sbt("wstage", [128, NST, 1024]); b_ws = [Buf("ws%d" % i) for i in range(NST)]
        bigt = sbt("bigt", [128, NBIG, TT]); BP = SlotPool(bigt, NBIG, "big")
        smt = sbt("smt", [128, NSM, 128]); SP = SlotPool(smt, NSM, "sm")
        rpt = sbt("rpt", [128, NRP, 256]); RP = SlotPool(rpt, NRP, "rp")
        rawt = sbt("rawt", [128, NRAW, TT + 3]); RAWP = SlotPool(rawt, NRAW, "raw")
        lxb = sbt("lxb", [128, 4, TT], BF16); b_lxb = [Buf("lxb%d" % j) for j in range(4)]
        lbdt = sbt("lbdt", [128, 1024], BF16); b_lbd = Buf("lbd")
        cst = sbt("cst", [128, NCST]); b_cst = Buf("cst")
        pv = sbt("pv", [128, L, NPV]); b_pv = Buf("pv")
        cTt = sbt("cTt", [128, 8]); b_cT = Buf("cT")
        modt = sbt("modt", [128, L, 64]); b_mod = Buf("mod")
        Sg = sbt("Sg", [128, L * 4, 128]); b_Sg = [Buf("Sg%d" % i) for i in range(L * 4)]
        Ss = sbt("Ss", [128, L, 512]); b_Ss = [Buf("Ss%d" % i) for i in range(L)]
        hl = sbt("hl", [128, L * 4, 2]); b_hl = [Buf("hl%d" % i) for i in range(L * 4)]
        tails = sbt("tails", [128, L * 24, 3]); b_tl = [Buf("tl%d" % i) for i in range(L * 24)]
        pst = [st.enter_context(nc.psum_tensor("ps%d" % i, [128, 512], F32)) for i in range(8)]

        class PSPool:
            def __init__(self):
                self.free = list(range(8))
                self.bufs = [Buf("ps%d" % i, excl=True) for i in range(8)]

            def alloc(self):
                if not self.free:
                    raise RuntimeError("PSUM exhausted")
                i = self.free.pop(0)
                s = Slot()
                s.pool, s.i, s.ap, s.buf = self, i, pst[i][:], self.bufs[i]
                return s

            def release(self, s):
                assert s.i not in self.free
                self.free.append(s.i)

        PS = PSPool()

        ident = cst[:, C_ID:C_ID + 128]
        ones = cst[:, C_ONES:C_ONES + 128]
        NEGUS = cst[:, C_NEGUS:C_NEGUS + 128]
        NEGUI = cst[:, C_NEGUI:C_NEGUI + 128]
        POSLS = cst[:, C_POSLS:C_POSLS + 128]

        def sel(row):
            k = SELROWS.index(row)
            return cst[:, C_SEL + k * 128:C_SEL + (k + 1) * 128]

        def bl(x):
            return [y if isinstance(y, Buf) else y.buf for y in x]

        def PE_mm(out, lhsT, rhs, start, stop, reads, writes):
            S.op("pe", lambda e: e.matmul(out=out, lhsT=lhsT, rhs=rhs, start=start, stop=stop), bl(reads), bl(writes), noinc=(not stop))

        def PE_tr(out, in_, idn, reads, writes):
            S.op("pe", lambda e: e.transpose(out=out, in_=in_, identity=idn), bl(reads) + [b_cst], bl(writes))

        def ACT(out, in_, func, reads, writes, **kw):
            S.op("act", lambda e: e.activation(out=out, in_=in_, func=func, **kw), bl(reads), bl(writes))

        def V(method, reads, writes, **kw):
            S.op("dve", lambda e: getattr(e, method)(**kw), bl(reads), bl(writes))

        def DMA(q, out, in_, reads, writes):
            S.op(q, lambda e: e.dma_start(out=out, in_=in_), bl(reads), bl(writes), dma=True)

        def pvc(l, off, n=1):
            return pv[:, l, off:off + n]

        class WStream:
            def __init__(self):
                self.recorded = []
                self.seq = None if order is None else [(ti, l, u) for ti in range(NT) for l in range(L) for u in order]
                self.issued = 0
                self.used = 0

            @staticmethod
            def width(u):
                if 38 <= u < 86 and (u - 38) % 2 == 0:
                    return 512
                return 1024

            def _issue(self, i, ti, l, u):
                s = i % NWS
                w = self.width(u)
                g = l * NU + u
                if ti == 0:
                    ss = i % NST
                    DMA("sp", wstage[:, ss, 0:w], wp_d[g, :, 0:w], [], [b_ws[ss]])
                    ceng = ("pool", "dve", "act")[i % 3]
                    if ceng == "act":
                        S.op("act", lambda e, s=s, ss=ss, w=w: e.activation(out=wring[:, s, 0:w], in_=wstage[:, ss, 0:w], func=AF.Copy), [b_ws[ss]], [b_wr[s]])
                    else:
                        S.op(ceng, lambda e, s=s, ss=ss, w=w: e.tensor_copy(out=wring[:, s, 0:w], in_=wstage[:, ss, 0:w]), [b_ws[ss]], [b_wr[s]])
                    if NT > 1:
                        DMA("sp", wbf_d[g, :, 0:w], wring[:, s, 0:w], [b_wr[s]], [b_wbf[g]])
                else:
                    DMA("sp", wring[:, s, 0:w], wbf_d[g, :, 0:w], [b_wbf[g]], [b_wr[s]])

            def get(self, l, u):
                i = self.used
                self.used += 1
                if self.seq is None:
                    self.recorded.append(u)
                    self._issue(i, 0, l, u)
                else:
                    assert self.seq[i][1:] == (l, u), (self.seq[i], l, u)
                    lim = min(i + NWS - 1, len(self.seq) - 1)
                    while self.issued <= lim:
                        self._issue(self.issued, *self.seq[self.issued])
                        self.issued += 1
                s = i % NWS
                return wring[:, s, :], b_wr[s]

        W = WStream()

        DMA("sp", cst[:], cst_d, [], [b_cst])
        DMA("sp", pv[:], pv_d.rearrange("l p n -> p l n"), [], [b_pv])
        DMA("sp", cTt[:], cT_d, [], [b_cT])
        V("memset", [], b_Sg, ap=Sg[:], constant=0.0)
        V("memset", [], b_Ss, ap=Ss[:], constant=0.0)
        V("memset", [], b_hl, ap=hl[:], constant=0.0)
        V("memset", [], b_tl, ap=tails[:], constant=0.0)
        V("memset", [], [b_mod], ap=modt[:], constant=0.0)
        ACT(cTt[:], cTt[:], AF.Silu, [b_cT], [b_cT])
        for l in range(L):
            pm = PS.alloc()
            for ob in range(48):
                for half in range(2):
                    ws = BP.alloc()
                    DMA("sp", ws.ap, wada_d[(l * 48 + ob) * 2 + half], [], [ws])
                    for cc in range(4):
                        c = half * 4 + cc
                        PE_mm(pm.ap[:, ob:ob + 1], ws.ap[:, cc * 128:(cc + 1) * 128], cTt[:, c:c + 1],
                              c == 0, c == 7, [ws, b_cT], [pm])
                    BP.release(ws)
            mo = SP.alloc()
            V("tensor_tensor", [pm, b_pv], [mo], out=mo.ap[:, 0:48], in0=pm.ap[:, 0:48], in1=pvc(l, PV_ADAB, 48), op=ALU.add)
            PS.release(pm)
            m = modt[:, l, :]
            V("scalar_tensor_tensor", [mo, b_pv], [b_mod], out=m[:, 0:8], in0=mo.ap[:, 8:16], scalar=1.0, in1=pvc(l, PV_NM, 8), op0=ALU.add, op1=ALU.mult)
            V("tensor_copy", [mo], [b_mod], out=m[:, 8:16], in_=mo.ap[:, 0:8])
            V("tensor_copy", [mo], [b_mod], out=m[:, 16:24], in_=mo.ap[:, 16:24])
            V("scalar_tensor_tensor", [mo, b_pv], [b_mod], out=m[:, 24:32], in0=mo.ap[:, 32:40], scalar=1.0, in1=pvc(l, PV_NMLP, 8), op0=ALU.add, op1=ALU.mult)
            V("tensor_copy", [mo], [b_mod], out=m[:, 32:40], in_=mo.ap[:, 24:32])
            V("tensor_copy", [mo], [b_mod], out=m[:, 40:48], in_=mo.ap[:, 40:48])
            SP.release(mo)
            ACT(m[:, 48:49], pvc(l, PV_RA), AF.Exp, [b_pv], [b_mod])
            V("tensor_scalar", [b_mod], [b_mod], out=m[:, 48:49], in0=m[:, 48:49], scalar1=-1.0, scalar2=None, op0=ALU.mult)
            ACT(m[:, 52:56], pvc(l, PV_LAM, 4), AF.Exp, [b_pv], [b_mod], scale=-1.0)
            ACT(m[:, 52:56], m[:, 52:56], AF.Ln, [b_mod], [b_mod], bias=1.0)
            V("tensor_scalar", [b_mod], [b_mod], out=m[:, 52:56], in0=m[:, 52:56], scalar1=-8.0, scalar2=None, op0=ALU.mult)

        def MOD(l, k):
            return modt[:, l, k * 8:(k + 1) * 8]

        def rstd_bc(scale, eps, srcs, src_bufs):
            pm = PS.alloc()
            n = len(srcs)
            for i, (a, b) in enumerate(zip(srcs, src_bufs)):
                sq = BP.alloc()
                ACT(sq.ap, a, AF.Square, [b], [sq])
                PE_mm(pm.ap, ones, sq.ap, i == 0, i == n - 1, [sq, b_cst], [pm])
                BP.release(sq)
            r = BP.alloc()
            ACT(r.ap, pm.ap, AF.Ln, [pm], [r], scale=scale, bias=eps)
            PS.release(pm)
            ACT(r.ap, r.ap, AF.Exp, [r], [r], scale=-0.5)
            return r

        def norm_to_hT(l, kA, kB):
            r = rstd_bc(1.0 / D, EPS, [xT[:, c, :] for c in range(8)], b_xT)
            A, Bc = MOD(l, kA), MOD(l, kB)
            for c in range(8):
                t = BP.alloc()
                V("tensor_tensor", [b_xT[c], r], [t], out=t.ap, in0=xT[:, c, :], in1=r.ap, op=ALU.mult)
                ACT(hT[:, c, :], t.ap, AF.Identity, [t, b_mod], [b_hT[c]], scale=A[:, c:c + 1], bias=Bc[:, c:c + 1])
                BP.release(t)
            BP.release(r)

        def proj(l, u, rhs_fn, rhs_bufs, nk=8):
            wap, wb = W.get(l, u)
            pm = PS.alloc()
            for c in range(nk):
                PE_mm(pm.ap, wap[:, c * 128:(c + 1) * 128], rhs_fn(c), c == 0, c == nk - 1, [wb, rhs_bufs[c]], [pm])
            return pm

        def inproj(l, u):
            return proj(l, u, lambda c: hT[:, c, :], b_hT)

        def conv(l, pm, tblk, wcol, bias, func, out):
            raw = RAWP.alloc()
            tb = b_tl[l * 24 + tblk]
            tl = tails[:, l * 24 + tblk, :]
            ACT(raw.ap[:, 3:TT + 3], pm.ap, AF.Copy, [pm], [raw])
            PS.release(pm)
            V("tensor_copy", [tb], [raw], out=raw.ap[:, 0:3], in_=tl)
            V("tensor_copy", [raw], [tb], out=tl, in_=raw.ap[:, TT:TT + 3])
            V("tensor_scalar", [raw, b_pv], [out], out=out.ap, in0=raw.ap[:, 0:TT], scalar1=wcol[:, 0:1], scalar2=None, op0=ALU.mult)
            for k in range(1, 4):
                V("scalar_tensor_tensor", [raw, b_pv, out], [out], out=out.ap, in0=raw.ap[:, k:TT + k], scalar=wcol[:, k:k + 1],
                  in1=out.ap, op0=ALU.mult, op1=ALU.add)
            RAWP.release(raw)
            if bias is None:
                ACT(out.ap, out.ap, func, [out], [out])
            else:
                ACT(out.ap, out.ap, func, [out, b_pv], [out], bias=bias)

        gstop = [int(_os.environ.get("GSTOP", "100000"))]

        def interleave(gens):
            gens = list(gens)
            while gens:
                for g in list(gens):
                    if gstop[0] <= 0:
                        raise _Stop()
                    gstop[0] -= 1
                    try:
                        next(g)
                    except StopIteration:
                        gens.remove(g)

        dbg_state = {"done": False}

        def layer(l):
            if stop_after <= 1:
                raise _Stop()
            norm_to_hT(l, 0, 1)
            m = modt[:, l, :]
            negA = m[:, 48:49]
            GC = BP.alloc(); EE = BP.alloc()
            TM = []

            def small_gen():
                pm = inproj(l, 0)
                RB = BP.alloc(); KD = BP.alloc(); CB = BP.alloc()
                for t_ in (RB, GC, EE, KD, CB):
                    V("memset", [], [t_], ap=t_.ap, constant=0.0)
                ACT(RB.ap[0:32, :], pm.ap[0:32, :], AF.Sigmoid, [pm], [RB])
                sl = slice(32, 64)
                ACT(CB.ap[sl, :], pm.ap[sl, :], AF.Exp, [pm, b_pv], [CB], bias=pv[sl, l, PV_RB:PV_RB + 1])
                PS.release(pm)
                yield
                ACT(CB.ap[sl, :], CB.ap[sl, :], AF.Ln, [CB], [CB], bias=1.0)
                V("tensor_scalar", [CB, b_mod], [KD], out=KD.ap[sl, :], in0=CB.ap[sl, :], scalar1=negA[sl, :], scalar2=None, op0=ALU.mult)
                for ci in range(4):
                    cs = slice(ci * 128, (ci + 1) * 128)
                    V("tensor_tensor_scan", [KD, b_cst], [GC], out=GC.ap[sl, cs], data0=ones[sl, :], data1=KD.ap[sl, cs],
                      initial=0.0, op0=ALU.mult, op1=ALU.add)
                yield
                ACT(EE.ap[sl, :], GC.ap[sl, :], AF.Exp, [GC], [EE])
                for ci in range(4):
                    cs = slice(ci * 128, (ci + 1) * 128)
                    ACT(KD.ap[sl, cs], GC.ap[sl, cs], AF.Exp, [GC], [KD], scale=-1.0, bias=GC.ap[sl, ci * 128 + 127:ci * 128 + 128])
                V("tensor_copy", [EE], [CB], out=CB.ap[32:36, :], in_=EE.ap[32:36, :])
                yield
                for ci in range(4):
                    cs = slice(ci * 128, (ci + 1) * 128)
                    pt = PS.alloc()
                    for k, src in enumerate((RB, CB, KD, GC)):
                        PE_tr(pt.ap[:, k * 128:(k + 1) * 128], src.ap[:, cs], ident, [src], [pt])
                    tm = RP.alloc()
                    ACT(tm.ap[:, 0:192].rearrange("p (b c) -> p b c", b=4), pt.ap.rearrange("p (b c) -> p b c", b=4)[:, :, 0:48], AF.Copy, [pt], [tm])
                    PS.release(pt)
                    TM.append(tm)
                    if ci == 1:
                        yield
                BP.release(RB); BP.release(CB); BP.release(KD)


            if stop_after <= 2:
                raise _Stop()
            def gdn_head(h):
                hs = l * 4 + h
                qc = BP.alloc(); kc = BP.alloc(); vc = BP.alloc()
                ub, ust = 1 + 4 * h, 1
                pmq = inproj(l, ub)
                conv(l, pmq, h, pv[:, l, PV_GCW + 4 * h:PV_GCW + 4 * h + 4], None, AF.Silu, qc)
                yield
                pmk = inproj(l, ub + ust)
                conv(l, pmk, 4 + h, pv[:, l, PV_GCW + 4 * (4 + h):PV_GCW + 4 * (4 + h) + 4], None, AF.Silu, kc)
                yield
                pmv = inproj(l, ub + 2 * ust)
                conv(l, pmv, 8 + h, pv[:, l, PV_GCW + 4 * (8 + h):PV_GCW + 4 * (8 + h) + 4], None, AF.Silu, vc)
                yield
                rq = rstd_bc(1.0, EPS, [qc.ap], [qc.buf])
                V("scalar_tensor_tensor", [qc, rq], [qc], out=qc.ap, in0=qc.ap, scalar=128.0 ** -0.5, in1=rq.ap, op0=ALU.mult, op1=ALU.mult)
                BP.release(rq)
                yield
                rk = rstd_bc(1.0, EPS, [kc.ap], [kc.buf])
                V("tensor_tensor", [kc, rk], [kc], out=kc.ap, in0=kc.ap, in1=rk.ap, op=ALU.mult)
                BP.release(rk)
                yield
                pe_ = PS.alloc()
                PE_mm(pe_.ap, sel(32 + h), EE.ap, True, True, [EE, b_cst], [pe_])
                qd = BP.alloc()
                if _os.environ.get("SKIPQD") is None:
                    V("tensor_tensor", [qc, pe_], [qd], out=qd.ap, in0=qc.ap, in1=pe_.ap, op=ALU.mult)
                gts = SP.alloc()
                V("tensor_copy", [pe_], [gts], out=gts.ap[:, 0:4], in_=pe_.ap.rearrange("p (c t) -> p c t", c=4)[:, :, 127])
                PS.release(pe_)
                oT = BP.alloc()
                yield
                for ci in range(4):
                    cs = slice(ci * 128, (ci + 1) * 128)
                    tm = TM[ci]
                    beta = tm.ap[:, h:h + 1]
                    Eg = tm.ap[:, 48 + 32 + h:48 + 32 + h + 1]
                    KDc = tm.ap[:, 96 + 32 + h:96 + 32 + h + 1]
                    gc_ = tm.ap[:, 144 + 32 + h:144 + 32 + h + 1]
                    pt = PS.alloc()
                    PE_tr(pt.ap[:, 0:128], kc.ap[:, cs], ident, [kc], [pt])
                    PE_tr(pt.ap[:, 128:256], vc.ap[:, cs], ident, [vc], [pt])
                    kb = SP.alloc(); kbg = SP.alloc(); kd = SP.alloc(); bv = SP.alloc()
                    V("tensor_scalar", [pt, tm], [kb], out=kb.ap, in0=pt.ap[:, 0:128], scalar1=beta, scalar2=None, op0=ALU.mult)
                    V("tensor_scalar", [kb, tm], [kbg], out=kbg.ap, in0=kb.ap, scalar1=Eg, scalar2=None, op0=ALU.mult)
                    ACT(kd.ap, pt.ap[:, 0:128], AF.Identity, [pt, tm], [kd], scale=KDc)
                    ACT(bv.ap, pt.ap[:, 128:256], AF.Identity, [pt, tm], [bv], scale=beta)
                    PE_tr(pt.ap[:, 256:384], kb.ap, ident, [kb], [pt])
                    kbT = SP.alloc()
                    ACT(kbT.ap, pt.ap[:, 256:384], AF.Copy, [pt], [kbT])
                    PS.release(pt)
                    SP.release(kb)
                    yield
                    pr = PS.alloc()
                    PE_mm(pr.ap[:, 0:128], kc.ap[:, cs], kbT.ap, True, True, [kc, kbT], [pr])
                    PE_mm(pr.ap[:, 128:256], kbT.ap, kc.ap[:, cs], True, True, [kc, kbT], [pr])
                    PE_mm(pr.ap[:, 256:384], kc.ap[:, cs], qc.ap[:, cs], True, True, [kc, qc], [pr])
                    PE_mm(pr.ap[:, 384:512], sel(32 + h), GC.ap[:, cs], True, True, [GC, b_cst], [pr])
                    SP.release(kbT)
                    dts = SP.alloc(); dl = SP.alloc()
                    V("scalar_tensor_tensor", [pr, tm, b_cst], [dts], out=dts.ap, in0=pr.ap[:, 384:512], scalar=gc_, in1=NEGUS, op0=ALU.subtract, op1=ALU.add)
                    V("scalar_tensor_tensor", [pr, tm, b_cst], [dl], out=dl.ap, in0=pr.ap[:, 384:512], scalar=gc_, in1=POSLS, op0=ALU.subtract, op1=ALU.add)
                    ACT(dts.ap, dts.ap, AF.Exp, [dts], [dts])
                    ACT(dl.ap, dl.ap, AF.Exp, [dl], [dl], scale=-1.0)
                    R = RP.alloc()
                    V("scalar_tensor_tensor", [pr, dts], [R], out=R.ap[:, 0:128], in0=pr.ap[:, 0:128], scalar=-1.0, in1=dts.ap, op0=ALU.mult, op1=ALU.mult)
                    V("tensor_copy", [b_cst], [R], out=R.ap[:, 128:256], in_=ident)
                    A = SP.alloc()
                    V("scalar_tensor_tensor", [pr, dl], [A], out=A.ap, in0=pr.ap[:, 128:256], scalar=-1.0, in1=dl.ap, op0=ALU.mult, op1=ALU.mult)
                    SP.release(dl)
                    V("tensor_tensor", [dts, b_cst], [dts], out=dts.ap, in0=dts.ap, in1=ident, op=ALU.add)
                    qkT = dts
                    V("tensor_tensor", [pr, qkT], [qkT], out=qkT.ap, in0=pr.ap[:, 256:384], in1=qkT.ap, op=ALU.mult)
                    PS.release(pr)
                    yield
                    for j in range(7):
                        pn = PS.alloc()
                        if j < 6:
                            PE_mm(pn.ap[:, 0:256], A.ap, R.ap, True, True, [A, R], [pn])
                            PE_mm(pn.ap[:, 256:384], R.ap[:, 0:128], A.ap, True, True, [A, R], [pn])
                            R2 = RP.alloc(); A2 = SP.alloc()
                            ACT(R2.ap[:, 0:128], pn.ap[:, 0:128], AF.Copy, [pn], [R2])
                            V("tensor_tensor", [pn, R], [R2], out=R2.ap[:, 128:256], in0=pn.ap[:, 128:256], in1=R.ap[:, 128:256], op=ALU.add)
                            ACT(A2.ap, pn.ap[:, 256:384], AF.Copy, [pn], [A2])
                            RP.release(R); SP.release(A)
                            R, A = R2, A2
                        else:
                            PE_mm(pn.ap[:, 0:128], A.ap, R.ap[:, 128:256], True, True, [A, R], [pn])
                            V("tensor_tensor", [pn, R], [R], out=R.ap[:, 128:256], in0=pn.ap[:, 0:128], in1=R.ap[:, 128:256], op=ALU.add)
                            SP.release(A)
                        PS.release(pn)
                        yield
                    Q = R.ap[:, 128:256]
                    pw = PS.alloc()
                    PE_mm(pw.ap[:, 0:128], kbg.ap, Q, True, True, [kbg, R], [pw])
                    nwT = SP.alloc()
                    ACT(nwT.ap, pw.ap[:, 0:128], AF.Copy, [pw], [nwT], scale=-1.0)
                    SP.release(kbg)
                    yield
                    Sh = Sg[:, hs, :]
                    PE_mm(pw.ap[:, 128:256], Q, bv.ap, True, False, [R, bv], [pw])
                    PE_mm(pw.ap[:, 128:256], nwT.ap, Sh, False, True, [nwT, b_Sg[hs]], [pw])
                    vn = SP.alloc()
                    ACT(vn.ap, pw.ap[:, 128:256], AF.Copy, [pw], [vn])
                    RP.release(R); SP.release(bv); SP.release(nwT)
                    yield
                    PE_mm(pw.ap[:, 256:384], Sh, qd.ap[:, cs], True, False, [b_Sg[hs], qd], [pw])
                    PE_mm(pw.ap[:, 256:384], vn.ap, qkT.ap, False, True, [vn, qkT], [pw])
                    PE_mm(pw.ap[:, 384:512], kd.ap, vn.ap, True, True, [kd, vn], [pw])
                    ACT(oT.ap[:, cs], pw.ap[:, 256:384], AF.Copy, [pw], [oT])
                    V("scalar_tensor_tensor", [b_Sg[hs], gts, pw], [b_Sg[hs]], out=Sh, in0=Sh, scalar=gts.ap[:, ci:ci + 1], in1=pw.ap[:, 384:512],
                      op0=ALU.mult, op1=ALU.add)
                    PS.release(pw)
                    SP.release(vn); SP.release(kd); SP.release(qkT)
                    yield
                SP.release(gts)
                BP.release(qd); BP.release(qc); BP.release(kc); BP.release(vc)
                pmz = inproj(l, ub + 3 * ust)
                sz = BP.alloc()
                ACT(sz.ap, pmz.ap, AF.Silu, [pmz], [sz])
                PS.release(pmz)
                yield
                r = rstd_bc(1.0 / 128, EPS, [oT.ap], [oT.buf])
                V("tensor_tensor", [oT, r], [oT], out=oT.ap, in0=oT.ap, in1=r.ap, op=ALU.mult)
                BP.release(r)
                V("scalar_tensor_tensor", [oT, b_pv, sz], [b_yT[h]], out=yT[:, h, :], in0=oT.ap, scalar=pvc(l, PV_GNW), in1=sz.ap, op0=ALU.mult, op1=ALU.mult)
                BP.release(oT); BP.release(sz)
                yield

            GDN_PHASE_MARK = None

            if stop_after <= 3:
                raise _Stop()
            def ssd_gen():
                sx = [BP.alloc() for _ in range(4)]
                sbm = [BP.alloc() for _ in range(2)]
                scm = [BP.alloc() for _ in range(2)]
                for j in range(4):
                    pm = inproj(l, 17 + j)
                    conv(l, pm, 12 + j, pv[:, l, PV_SCW + 4 * j:PV_SCW + 4 * j + 4], pvc(l, PV_SCB + j), AF.Silu, sx[j])
                    yield
                for g in range(2):
                    pm = inproj(l, 21 + g)
                    conv(l, pm, 16 + g, pv[:, l, PV_SCW + 4 * (4 + g):PV_SCW + 4 * (4 + g) + 4], pvc(l, PV_SCB + 4 + g), AF.Silu, sbm[g])
                    yield
                for g in range(2):
                    pm = inproj(l, 23 + g)
                    conv(l, pm, 18 + g, pv[:, l, PV_SCW + 4 * (6 + g):PV_SCW + 4 * (6 + g) + 4], pvc(l, PV_SCB + 6 + g), AF.Silu, scm[g])
                    yield
                yb = [BP.alloc() for _ in range(4)]
                Sst = Ss[:, l, :]

                def chunk_pre(ci, res):
                    cs = slice(ci * 128, (ci + 1) * 128)
                    tm = TM[ci]
                    dt8 = tm.ap[:, 48 + 40:48 + 48]
                    kd8 = tm.ap[:, 96 + 40:96 + 48]
                    pt = PS.alloc()
                    for j in range(4):
                        PE_tr(pt.ap[:, j * 128:(j + 1) * 128], sx[j].ap[:, cs], ident, [sx[j]], [pt])
                    xdt = BP.alloc(); xdd = BP.alloc()
                    V("tensor_tensor", [pt, tm], [xdt], out=xdt.ap.rearrange("p (h q) -> p h q", h=8), in0=pt.ap.rearrange("p (h q) -> p h q", h=8),
                      in1=dt8.unsqueeze(2).to_broadcast([128, 8, 64]), op=ALU.mult)
                    PS.release(pt)
                    V("tensor_tensor", [xdt, tm], [xdd], out=xdd.ap.rearrange("p (h q) -> p h q", h=8), in0=xdt.ap.rearrange("p (h q) -> p h q", h=8),
                      in1=kd8.unsqueeze(2).to_broadcast([128, 8, 64]), op=ALU.mult)
                    yield
                    pt = PS.alloc()
                    for g in range(2):
                        PE_tr(pt.ap[:, g * 128:(g + 1) * 128], sbm[g].ap[:, cs], ident, [sbm[g]], [pt])
                        PE_mm(pt.ap[:, 256 + g * 128:256 + (g + 1) * 128], sbm[g].ap[:, cs], scm[g].ap[:, cs], True, True, [sbm[g], scm[g]], [pt])
                    bmt = RP.alloc()
                    ACT(bmt.ap, pt.ap[:, 0:256], AF.Copy, [pt], [bmt])
                    cbm = RP.alloc()
                    ACT(cbm.ap, pt.ap[:, 256:512], AF.Copy, [pt], [cbm])
                    PS.release(pt)
                    yield
                    Wl = [BP.alloc(), BP.alloc()]
                    Eb = [BP.alloc(), BP.alloc()]
                    cdec = SP.alloc()
                    for b2 in range(2):
                        pa = PS.alloc()
                        for h4 in range(4):
                            PE_mm(pa.ap[:, h4 * 128:(h4 + 1) * 128], sel(40 + b2 * 4 + h4), GC.ap[:, cs], True, True, [GC, b_cst], [pa])
                        for h4 in range(4):
                            hh, o = b2 * 4 + h4, h4 * 128
                            V("scalar_tensor_tensor", [pa, tm, b_cst], [Wl[b2]], out=Wl[b2].ap[:, o:o + 128], in0=pa.ap[:, o:o + 128],
                              scalar=tm.ap[:, 144 + 40 + hh:144 + 40 + hh + 1], in1=NEGUI, op0=ALU.subtract, op1=ALU.add)
                        ACT(Eb[b2].ap, pa.ap, AF.Exp, [pa], [Eb[b2]])
                        PS.release(pa)
                        ACT(Wl[b2].ap, Wl[b2].ap, AF.Exp, [Wl[b2]], [Wl[b2]])
                        yield
                        V("tensor_tensor", [Wl[b2], cbm], [Wl[b2]], out=Wl[b2].ap.rearrange("p (h c) -> p h c", h=4),
                          in0=Wl[b2].ap.rearrange("p (h c) -> p h c", h=4),
                          in1=cbm.ap[:, b2 * 128:(b2 + 1) * 128].unsqueeze(1).to_broadcast([128, 4, 128]), op=ALU.mult)
                        V("tensor_copy", [Eb[b2]], [cdec], out=cdec.ap[:, b2 * 4:(b2 + 1) * 4],
                          in_=Eb[b2].ap.rearrange("p (h c) -> p h c", h=4)[:, :, 127])
                        V("tensor_tensor", [Eb[b2], scm[b2]], [Eb[b2]], out=Eb[b2].ap.rearrange("p (h c) -> p h c", h=4),
                          in0=Eb[b2].ap.rearrange("p (h c) -> p h c", h=4),
                          in1=scm[b2].ap[:, cs].unsqueeze(1).to_broadcast([128, 4, 128]), op=ALU.mult)
                        yield
                    RP.release(cbm)
                    res[ci] = (xdt, xdd, bmt, Wl, Eb, cdec)

                def chunk_fin(ci, res):
                    cs = slice(ci * 128, (ci + 1) * 128)
                    xdt, xdd, bmt, Wl, Eb, cdec = res[ci]
                    py = PS.alloc()
                    for hh in range(8):
                        b2, o = hh // 4, (hh % 4) * 128
                        blk, half = hh // 2, hh % 2
                        outp = py.ap[half * 64:(half + 1) * 64, blk * 128:(blk + 1) * 128]
                        PE_mm(outp, xdt.ap[:, hh * 64:(hh + 1) * 64], Wl[b2].ap[:, o:o + 128], True, False, [xdt, Wl[b2]], [py])
                        PE_mm(outp, Sst[:, hh * 64:(hh + 1) * 64], Eb[b2].ap[:, o:o + 128], False, True, [b_Ss[l], Eb[b2]], [py])
                    pu = PS.alloc()
                    for g in range(2):
                        PE_mm(pu.ap[:, g * 256:(g + 1) * 256], bmt.ap[:, g * 128:(g + 1) * 128], xdd.ap[:, g * 256:(g + 1) * 256], True, True, [bmt, xdd], [pu])
                    for blk in range(4):
                        V("scalar_tensor_tensor", [sx[blk], b_pv, py], [yb[blk]], out=yb[blk].ap[:, cs], in0=sx[blk].ap[:, cs],
                          scalar=pvc(l, PV_SD + blk), in1=py.ap[:, blk * 128:(blk + 1) * 128], op0=ALU.mult, op1=ALU.add)
                    PS.release(py)
                    V("tensor_tensor", [b_Ss[l], cdec], [b_Ss[l]], out=Sst.rearrange("p (h q) -> p h q", h=8), in0=Sst.rearrange("p (h q) -> p h q", h=8),
                      in1=cdec.ap[:, 0:8].unsqueeze(2).to_broadcast([128, 8, 64]), op=ALU.mult)
                    V("tensor_tensor", [b_Ss[l], pu], [b_Ss[l]], out=Sst, in0=Sst, in1=pu.ap, op=ALU.add)
                    PS.release(pu)
                    BP.release(Wl[0]); BP.release(Wl[1]); BP.release(Eb[0]); BP.release(Eb[1]); BP.release(xdt)
                    RP.release(bmt); SP.release(cdec); BP.release(xdd)

                res = {}
                for pair in ((0, 1), (2, 3)):
                    gens = [chunk_pre(ci, res) for ci in pair]
                    while gens:
                        for g_ in list(gens):
                            try:
                                next(g_)
                            except StopIteration:
                                gens.remove(g_)
                        yield
                    for ci in pair:
                        chunk_fin(ci, res)
                        yield
                for s_ in sx + sbm + scm:
                    BP.release(s_)
                BP.release(GC); BP.release(EE)
                for tm in TM:
                    RP.release(tm)
                for j in range(4):
                    pm = inproj(l, 25 + j)
                    sz = BP.alloc()
                    ACT(sz.ap, pm.ap, AF.Silu, [pm], [sz])
                    PS.release(pm)
                    V("tensor_tensor", [yb[j], sz], [yb[j]], out=yb[j].ap, in0=yb[j].ap, in1=sz.ap, op=ALU.mult)
                    BP.release(sz)
                    yield
                for g in range(2):
                    r = rstd_bc(1.0 / 256, EPS, [yb[2 * g].ap, yb[2 * g + 1].ap], [yb[2 * g].buf, yb[2 * g + 1].buf])
                    for j in (2 * g, 2 * g + 1):
                        V("scalar_tensor_tensor", [yb[j], b_pv, r], [b_yT[4 + j]], out=yT[:, 4 + j, :], in0=yb[j].ap, scalar=pvc(l, PV_SNW + j), in1=r.ap,
                          op0=ALU.mult, op1=ALU.mult)
                    BP.release(r)
                    yield
                for s_ in yb:
                    BP.release(s_)


            def lru_gen():
                wl_ap, wl_b = W.get(l, 37)
                S.op("pool", lambda e: e.tensor_copy(out=lbdt[:], in_=wl_ap), [wl_b], [b_lbd])
                lbd_ap, lbd_b = lbdt[:], b_lbd
                yield
                for j in range(4):
                    xc = BP.alloc()
                    pm = inproj(l, 29 + 2 * j)
                    conv(l, pm, 20 + j, pv[:, l, PV_LCW + 4 * j:PV_LCW + 4 * j + 4], pvc(l, PV_LCB + j), AF.Identity, xc)
                    V("tensor_copy", [xc], [b_lxb[j]], out=lxb[:, j, :], in_=xc.ap)
                    yield
                    pg = inproj(l, 30 + 2 * j)
                    gg = BP.alloc()
                    ACT(gg.ap, pg.ap, AF.Gelu_apprx_tanh, [pg], [gg])
                    PS.release(pg)
                    yield
                    pa_ = PS.alloc(); pi_ = PS.alloc()
                    PE_mm(pa_.ap, lbd_ap[:, j * 128:(j + 1) * 128], lxb[:, j, :], True, True, [lbd_b, b_lxb[j]], [pa_])
                    PE_mm(pi_.ap, lbd_ap[:, (4 + j) * 128:(5 + j) * 128], lxb[:, j, :], True, True, [lbd_b, b_lxb[j]], [pi_])
                    ra = BP.alloc(); ri = BP.alloc()
                    ACT(ra.ap, pa_.ap, AF.Sigmoid, [pa_, b_pv], [ra], bias=pvc(l, PV_LBA + j))
                    ACT(ri.ap, pi_.ap, AF.Sigmoid, [pi_, b_pv], [ri], bias=pvc(l, PV_LBX + j))
                    PS.release(pa_); PS.release(pi_)
                    yield
                    ACT(ra.ap, ra.ap, AF.Exp, [ra, b_mod], [ra], scale=m[:, 52 + j:53 + j])
                    mu = BP.alloc()
                    V("tensor_tensor", [ra], [mu], out=mu.ap, in0=ra.ap, in1=ra.ap, op=ALU.mult)
                    ACT(mu.ap, mu.ap, AF.Sqrt, [mu], [mu], scale=-1.0, bias=1.0)
                    V("tensor_tensor", [ri, xc], [ri], out=ri.ap, in0=ri.ap, in1=xc.ap, op=ALU.mult)
                    yield
                    V("tensor_tensor", [ri, mu], [mu], out=mu.ap, in0=ri.ap, in1=mu.ap, op=ALU.mult)
                    hb = b_hl[l * 4 + j]
                    V("tensor_tensor_scan", [ra, mu, hb], [xc], out=xc.ap, data0=ra.ap, data1=mu.ap, initial=hl[:, l * 4 + j, 1:2], op0=ALU.mult, op1=ALU.add)
                    V("tensor_copy", [xc], [hb], out=hl[:, l * 4 + j, :], in_=xc.ap[:, TT - 2:TT])
                    V("tensor_tensor", [xc, gg], [b_yT[8 + j]], out=yT[:, 8 + j, :], in0=xc.ap, in1=gg.ap, op=ALU.mult)
                    BP.release(ra); BP.release(ri); BP.release(mu); BP.release(xc); BP.release(gg)
                    yield

            interleave([small_gen(), gdn_head(0), gdn_head(1), gdn_head(2), gdn_head(3), lru_gen()])
            interleave([ssd_gen()])

            if dbg and not dbg_state["done"]:
                dbg_state["done"] = True
                for k in range(12):
                    t = BP.alloc()
                    V("tensor_copy", [b_yT[k]], [t], out=t.ap, in_=yT[:, k, :])
                    DMA("sp", dbg_d[:, k, :], t.ap, [t], [])
                    BP.release(t)

            if stop_after <= 5:
                raise _Stop()
            for ob in range(8):
                acc = BP.alloc()
                for r in range(3):
                    u = 38 + (ob * 3 + r) * 2
                    pb = proj(l, u, lambda c, r=r: yT[:, 4 * r + c, :], b_yT[4 * r:4 * r + 4], nk=4)
                    pg = inproj(l, u + 1)
                    sg = BP.alloc()
                    ACT(sg.ap, pg.ap, AF.Sigmoid, [pg], [sg])
                    PS.release(pg)
                    if r == 0:
                        V("tensor_tensor", [pb, sg], [acc], out=acc.ap, in0=pb.ap, in1=sg.ap, op=ALU.mult)
                    else:
                        V("tensor_tensor", [pb, sg], [sg], out=sg.ap, in0=pb.ap, in1=sg.ap, op=ALU.mult)
                        if r == 1:
                            V("tensor_tensor", [acc, sg], [acc], out=acc.ap, in0=acc.ap, in1=sg.ap, op=ALU.add)
                        else:
                            V("tensor_tensor", [acc, sg], [b_mT[ob]], out=mT[:, ob, :], in0=acc.ap, in1=sg.ap, op=ALU.add)
                    PS.release(pb)
                    BP.release(sg)
                BP.release(acc)
            G1 = MOD(l, 2)
            for ob in range(8):
                pm = proj(l, 86 + ob, lambda c: mT[:, c, :], b_mT)
                V("scalar_tensor_tensor", [pm, b_mod, b_xT[ob]], [b_xT[ob]], out=xT[:, ob, :], in0=pm.ap, scalar=G1[:, ob:ob + 1], in1=xT[:, ob, :],
                  op0=ALU.mult, op1=ALU.add)
                PS.release(pm)

            if stop_after <= 6:
                raise _Stop()
            norm_to_hT(l, 3, 4)
            asl = [BP.alloc() for _ in range(16)]
            aT = [s_.ap.bitcast(BF16).rearrange("p (a t) -> p a t", a=2) for s_ in asl]

            def aT_ap(c):
                return aT[c // 2][:, c % 2, :]

            for ob in range(32):
                pm = inproj(l, 94 + ob)
                sq = BP.alloc()
                ACT(sq.ap, pm.ap, AF.Relu, [pm], [sq])
                V("tensor_tensor", [sq, pm], [asl[ob // 2]], out=aT_ap(ob), in0=sq.ap, in1=pm.ap, op=ALU.mult)
                PS.release(pm)
                BP.release(sq)
            G2 = MOD(l, 5)
            for ob in range(8):
                pm = PS.alloc()
                for qtr in range(4):
                    wap, wb = W.get(l, 126 + ob * 4 + qtr)
                    for cc in range(8):
                        c = qtr * 8 + cc
                        PE_mm(pm.ap, wap[:, cc * 128:(cc + 1) * 128], aT_ap(c), c == 0, c == 31, [wb, asl[c // 2]], [pm])
                V("scalar_tensor_tensor", [pm, b_mod, b_xT[ob]], [b_xT[ob]], out=xT[:, ob, :], in0=pm.ap, scalar=G2[:, ob:ob + 1], in1=xT[:, ob, :],
                  op0=ALU.mult, op1=ALU.add)
                PS.release(pm)
            for s_ in asl:
                BP.release(s_)

        for ti in range(NT):
            xs = [BP.alloc() for _ in range(8)]
            for s_ in range(4):
                for hf in range(2):
                    DMA("sp", xs[2 * s_ + hf].ap, x_d[ti * TT + s_ * 128:ti * TT + (s_ + 1) * 128, hf * 512:(hf + 1) * 512], [], [xs[2 * s_ + hf]])
            for c in range(8):
                pt = PS.alloc()
                for s_ in range(4):
                    src = xs[2 * s_ + c // 4]
                    PE_tr(pt.ap[:, s_ * 128:(s_ + 1) * 128], src.ap[:, (c % 4) * 128:(c % 4 + 1) * 128], ident, [src], [pt])
                ACT(xT[:, c, :], pt.ap, AF.Copy, [pt], [b_xT[c]])
                PS.release(pt)
            for s_ in xs:
                BP.release(s_)
            try:
                for l in range(L):
                    layer(l)
            except _Stop:
                for P_ in (BP, SP, RP, RAWP, PS):
                    P_.free = list(range(len(P_.bufs)))
            r = rstd_bc(1.0 / D, EPS, [xT[:, c, :] for c in range(8)], b_xT)
            xo = [BP.alloc() for _ in range(8)]
            fn = [BP.alloc() for _ in range(8)]
            for c in range(8):
                V("scalar_tensor_tensor", [b_xT[c], b_pv, r], [fn[c]], out=fn[c].ap, in0=xT[:, c, :], scalar=pv[:, 0, PV_FNW + c:PV_FNW + c + 1], in1=r.ap,
                  op0=ALU.mult, op1=ALU.mult)
            BP.release(r)
            for s_ in range(4):
                for hf in range(2):
                    pt = PS.alloc()
                    for cc in range(4):
                        c = hf * 4 + cc
                        PE_tr(pt.ap[:, cc * 128:(cc + 1) * 128], fn[c].ap[:, s_ * 128:(s_ + 1) * 128], ident, [fn[c]], [pt])
                    dst = xo[2 * s_ + hf]
                    ACT(dst.ap, pt.ap, AF.Copy, [pt], [dst])
                    PS.release(pt)
                    DMA("sp", y_d[ti * TT + s_ * 128:ti * TT + (s_ + 1) * 128, hf * 512:(hf + 1) * 512], dst.ap, [dst], [])
            for s_ in xo + fn:
                BP.release(s_)
        S.final_wait("sp")
        build_program.recorded = W.recorded
        build_program.stats = dict(n_ops=S.n_ops, per_eng={e: len(S.ops[e]) for e in ENGS},
                                   minfree=dict(big=BP.minfree, sm=SP.minfree, rp=RP.minfree))
        S.emit()
    return nc


def _unit(Wm, cols, nk=8):
    K = Wm.shape[0]
    out = np.zeros((128, nk, 128), np.float32)
    cols = np.asarray(cols)
    ok = cols >= 0
    sub = Wm[:nk * 128][:, cols[ok]]
    out[:, :, ok] = sub.reshape(nk, 128, -1).transpose(1, 0, 2)
    return out.reshape(128, nk * 128)


def _colvec(v):
    v = np.asarray(v, np.float32)
    return np.ascontiguousarray(v.reshape(-1, 128).T)


def make_consts():
    c = np.zeros((128, NCST), np.float32)
    p = np.arange(128)[:, None]
    f = np.arange(128)[None, :]
    c[:, C_ID:C_ID + 128] = (p == f)
    c[:, C_ONES:C_ONES + 128] = 1.0
    c[:, C_NEGUS:C_NEGUS + 128] = np.where(f > p, 0.0, -1e30)
    c[:, C_NEGUI:C_NEGUI + 128] = np.where(f >= p, 0.0, -1e30)
    c[:, C_POSLS:C_POSLS + 128] = np.where(f < p, 0.0, 1e30)
    for k, r in enumerate(SELROWS):
        c[r, C_SEL + k * 128:C_SEL + (k + 1) * 128] = 1.0
    return c


def pack_layer_weights(w_in, w_branch, w_out, w_up, w_down, lru_w_a, lru_w_x):
    units = np.zeros((NU, 128, 1024), np.float32)
    ar = np.arange(128)
    small = -np.ones(128, np.int64)
    small[0:4] = 2048 + np.arange(4)
    small[32:36] = 2052 + np.arange(4)
    small[40:48] = 3592 + np.arange(8)
    units[0] = _unit(w_in, small)
    for h in range(4):
        ub, ust = 1 + 4 * h, 1
        units[ub] = _unit(w_in, 0 + 128 * h + ar)
        units[ub + ust] = _unit(w_in, 512 + 128 * h + ar)
        units[ub + 2 * ust] = _unit(w_in, 1024 + 128 * h + ar)
        units[ub + 3 * ust] = _unit(w_in, 1536 + 128 * h + ar)
    for j in range(4):
        units[17 + j] = _unit(w_in, 2056 + 128 * j + ar)
        units[25 + j] = _unit(w_in, 2568 + 128 * j + ar)
        units[29 + 2 * j] = _unit(w_in, 3600 + 128 * j + ar)
        units[30 + 2 * j] = _unit(w_in, 4112 + 128 * j + ar)
    for g in range(2):
        units[21 + g] = _unit(w_in, 3080 + 128 * g + ar)
        units[23 + g] = _unit(w_in, 3336 + 128 * g + ar)
    bd = np.zeros((128, 8, 128), np.float32)
    for j in range(4):
        for hb in range(2):
            blk = 2 * j + hb
            bd[hb * 64:(hb + 1) * 64, j, hb * 64:(hb + 1) * 64] = lru_w_a[blk]
            bd[hb * 64:(hb + 1) * 64, 4 + j, hb * 64:(hb + 1) * 64] = lru_w_x[blk]
    units[37] = bd.reshape(128, 1024)
    for ob in range(8):
        for r in range(3):
            u = 38 + (ob * 3 + r) * 2
            units[u, :, 0:512] = _unit(w_branch[r], ob * 128 + ar, nk=4)
            units[u + 1] = _unit(w_in, 4624 + r * 1024 + ob * 128 + ar)
        units[86 + ob] = _unit(w_out, ob * 128 + ar)
    for ob in range(32):
        units[94 + ob] = _unit(w_up, ob * 128 + ar)
    for ob in range(8):
        for qtr in range(4):
            units[126 + ob * 4 + qtr] = _unit(w_down[qtr * 1024:(qtr + 1) * 1024], ob * 128 + ar)
    return units


def pack_ada(ada_w_l):
    out = np.zeros((N_ADA, 128, 512), np.float32)
    ar = np.arange(128)
    for ob in range(48):
        for half in range(2):
            out[ob * 2 + half] = _unit(ada_w_l[half * 512:(half + 1) * 512], ob * 128 + ar, nk=4)
    return out


def pack_pv(l, p):
    v = np.zeros((128, NPV), np.float32)
    v[:, PV_NM:PV_NM + 8] = _colvec(p["norm_mix"][l])
    v[:, PV_NMLP:PV_NMLP + 8] = _colvec(p["norm_mlp"][l])
    v[:, PV_ADAB:PV_ADAB + 48] = _colvec(p["ada_b"][l])
    gcw = p["gdn_conv_w"][l]
    for b in range(12):
        v[:, PV_GCW + 4 * b:PV_GCW + 4 * b + 4] = gcw[:, b * 128:(b + 1) * 128].T
    scw = p["ssd_conv_w"][l]
    for b in range(8):
        v[:, PV_SCW + 4 * b:PV_SCW + 4 * b + 4] = scw[:, b * 128:(b + 1) * 128].T
    v[:, PV_SCB:PV_SCB + 8] = _colvec(p["ssd_conv_b"][l])
    lcw = p["lru_conv_w"][l]
    for b in range(4):
        v[:, PV_LCW + 4 * b:PV_LCW + 4 * b + 4] = lcw[:, b * 128:(b + 1) * 128].T
    v[:, PV_LCB:PV_LCB + 4] = _colvec(p["lru_conv_b"][l])
    v[:, PV_LBA:PV_LBA + 4] = _colvec(p["lru_b_a"][l])
    v[:, PV_LBX:PV_LBX + 4] = _colvec(p["lru_b_x"][l])
    v[:, PV_LAM:PV_LAM + 4] = _colvec(p["lru_lambda"][l])
    v[:, PV_GNW] = p["gdn_norm"][l]
    v[:, PV_SNW:PV_SNW + 4] = _colvec(p["ssd_norm"][l])
    v[:, PV_SD:PV_SD + 4] = _colvec(np.repeat(np.asarray(p["ssd_d"][l]), 64))
    v[32:36, PV_RB] = p["gdn_dt_bias"][l]
    v[40:48, PV_RB] = p["ssd_dt_bias"][l]
    v[32:36, PV_RA] = p["gdn_a_log"][l]
    v[40:48, PV_RA] = p["ssd_a_log"][l]
    v[:, PV_FNW:PV_FNW + 8] = _colvec(p["final_norm"])
    return v


def prepare_shared(p, L):
    p = {k: np.asarray(v, np.float32) for k, v in p.items()}
    wpack = np.concatenate([pack_layer_weights(p["w_in"][l], p["w_branch"][l], p["w_out"][l], p["w_up"][l], p["w_down"][l],
                                               p["lru_w_a"][l], p["lru_w_x"][l]) for l in range(L)], axis=0)
    wada = np.concatenate([pack_ada(p["ada_w"][l]) for l in range(L)], axis=0)
    pvv = np.stack([pack_pv(l, p) for l in range(L)], axis=0)
    return {"wpack": wpack, "wada": wada, "pv": pvv, "cst": make_consts()}


_ORDER = []


def unit_order():
    if not _ORDER:
        build_program(TT, 1, order=None)
        rec = list(build_program.recorded)
        assert sorted(rec) == list(range(NU)), len(rec)
        _ORDER.extend(rec)
    return list(_ORDER)


def kernel(**inputs):
    x = np.asarray(inputs["x"], np.float32)
    c = np.asarray(inputs["c"], np.float32)
    B, T, _ = x.shape
    L = inputs["w_in"].shape[0]
    params = {k: v for k, v in inputs.items() if k not in ("x", "c")}
    shared = prepare_shared(params, L)
    nc = build_program(T, L, order=unit_order())
    in_maps = []
    for b in range(B):
        m = dict(shared)
        m["x"] = np.ascontiguousarray(x[b])
        m["cT"] = _colvec(c[b])
        in_maps.append(m)
    res = run_bass_kernel_spmd(nc, in_maps, core_ids=list(range(B)))
    return np.stack([np.asarray(r["y"], np.float32) for r in res.results], axis=0)
```

```python
import contextlib
import os as _os
import numpy as np
import concourse.bass as bass
import concourse.mybir as mybir
from concourse.bass_utils import run_bass_kernel_spmd

F32 = mybir.dt.float32
BF16 = mybir.dt.bfloat16
AF = mybir.ActivationFunctionType
ALU = mybir.AluOpType

D = 1024
TT = 512
NU = 158
N_ADA = 96
EPS = 1e-6
GDN_IL = 2
NPV = 208
NCST = 5 * 128 + 12 * 128
PV_NM, PV_NMLP, PV_ADAB, PV_GCW, PV_SCW, PV_SCB, PV_LCW, PV_LCB = 0, 8, 16, 64, 112, 144, 152, 168
PV_LBA, PV_LBX, PV_LAM, PV_GNW, PV_SNW, PV_SD, PV_RB, PV_RA, PV_FNW = 172, 176, 180, 184, 185, 189, 193, 194, 195
C_ID, C_ONES, C_NEGUS, C_NEGUI, C_POSLS, C_SEL = 0, 128, 256, 384, 512, 640
SELROWS = [32, 33, 34, 35, 40, 41, 42, 43, 44, 45, 46, 47]

ENGS = ("pe", "act", "dve", "pool", "sp")
N_DMA_SEMS = 24


class Buf:
    __slots__ = ("name", "w", "r", "excl")

    def __init__(self, name="", excl=False):
        self.name = name
        self.w = None
        self.r = []
        self.excl = excl


class Sched:
    def __init__(self, nc):
        self.nc = nc
        self.ops = {e: [] for e in ENGS}
        self.cnt = {e: 0 for e in ENGS}
        self.waited = {e: {} for e in ENGS}
        self.dma_i = 0
        self.dma_cnt = [0] * N_DMA_SEMS
        self.sw_gen = {}
        self.n_ops = 0

    def _need(self, eng, tok, waits, is_raw):
        if tok is None:
            return
        semkey, val, teng = tok
        if teng == eng and semkey == eng and not is_raw and eng == "pe":
            return
        if self.waited[eng].get(semkey, 0) >= val:
            return
        if waits.get(semkey, 0) < val:
            waits[semkey] = val

    def op(self, eng, fn, reads=(), writes=(), dma=False, swslot=None, noinc=False):
        waits = {}
        for b in reads:
            self._need(eng, b.w, waits, True)
            if b.excl:
                for t in b.r:
                    if t[2] != eng:
                        self._need(eng, t, waits, False)
        for b in writes:
            self._need(eng, b.w, waits, False)
            for t in b.r:
                self._need(eng, t, waits, False)
        clear = None
        if swslot is not None:
            gen = self.sw_gen.get(swslot, 0)
            self.sw_gen[swslot] = gen + 1
            tok = ("sw%d:%d" % (swslot, gen), 16, "dma")
            inc = ("sw%d" % swslot, 16)
            if gen > 0:
                clear = "sw%d" % swslot
        elif dma:
            k = self.dma_i % N_DMA_SEMS
            self.dma_i += 1
            if self.dma_cnt[k] > 0:
                self._need(eng, ("dma%d" % k, 16 * self.dma_cnt[k], "dma"), waits, False)
            self.dma_cnt[k] += 1
            tok = ("dma%d" % k, 16 * self.dma_cnt[k], "dma")
            inc = ("dma%d" % k, 16)
        elif noinc:
            tok = (eng, self.cnt[eng] + 1, eng)
            inc = None
        else:
            self.cnt[eng] += 1
            tok = (eng, self.cnt[eng], eng)
            inc = (eng, 1)
        for sk, v in waits.items():
            self.waited[eng][sk] = v
        self.ops[eng].append((fn, tuple(waits.items()), inc, clear))
        for b in reads:
            b.r.append(tok)
            if len(b.r) > 64:
                b.r = _compact(b.r)
        for b in writes:
            b.w = tok
            b.r = []
        self.n_ops += 1
        return tok

    def final_wait(self, eng="sp"):
        waits = {}
        for e in ENGS:
            if self.cnt[e] and e != eng:
                self._need(eng, (e, self.cnt[e], e), waits, False)
        for k in range(N_DMA_SEMS):
            if self.dma_cnt[k]:
                self._need(eng, ("dma%d" % k, 16 * self.dma_cnt[k], "dma"), waits, False)
        self.ops[eng].append((None, tuple(waits.items()), None, None))

    def emit(self):
        nc = self.nc
        with contextlib.ExitStack() as st:
            sems = {}
            for e in ENGS:
                sems[e] = st.enter_context(nc.semaphore("s_" + e))
            for k in range(N_DMA_SEMS):
                sems["dma%d" % k] = st.enter_context(nc.semaphore("s_dma%d" % k))
            for k in self.sw_gen:
                sems["sw%d" % k] = st.enter_context(nc.semaphore("s_sw%d" % k))
            block = st.enter_context(nc.Block())

            def run(eng_name):
                def body(eng):
                    for fn, waits, inc, clear in self.ops[eng_name]:
                        for sk, v in waits:
                            eng.wait_ge(sems[sk.split(":")[0]], v)
                        if fn is None:
                            continue
                        if clear is not None:
                            eng.wait_ge(sems[clear], 16)
                            eng.sem_clear(sems[clear])
                        ins = fn(eng)
                        if inc is not None:
                            ins.then_inc(sems[inc[0]], inc[1])
                return body

            block.tensor(run("pe"))
            block.scalar(run("act"))
            block.vector(run("dve"))
            block.gpsimd(run("pool"))
            block.sync(run("sp"))


def _compact(toks):
    best = {}
    for t in toks:
        if t[0] not in best or best[t[0]][1] < t[1]:
            best[t[0]] = t
    return list(best.values())


class Slot:
    __slots__ = ("pool", "i", "ap", "buf")


class SlotPool:
    def __init__(self, tile, n, name):
        self.tile = tile
        self.n = n
        self.free = list(range(n))
        self.bufs = [Buf("%s%d" % (name, i)) for i in range(n)]
        self.name = name
        self.minfree = n

    def alloc(self, idx=None):
        if not self.free:
            raise RuntimeError("pool %s exhausted" % self.name)
        if idx is None:
            i = self.free.pop(0)
        else:
            self.free.remove(idx)
            i = idx
        self.minfree = min(self.minfree, len(self.free))
        s = Slot()
        s.pool, s.i, s.ap, s.buf = self, i, self.tile[:, i, :], self.bufs[i]
        return s

    def release(self, s):
        assert s.i not in self.free
        self.free.append(s.i)


class _Stop(Exception):
    pass


def build_program(T, L, dbg=False, stop_after=99, order=None):
    assert T % TT == 0
    NT = T // TT
    nc = bass.Bass("TRN2", target_bir_lowering=False)
    x_d = nc.dram_tensor("x", [T, D], F32, kind="ExternalInput").ap()
    cT_d = nc.dram_tensor("cT", [128, 8], F32, kind="ExternalInput").ap()
    pv_d = nc.dram_tensor("pv", [L, 128, NPV], F32, kind="ExternalInput").ap()
    cst_d = nc.dram_tensor("cst", [128, NCST], F32, kind="ExternalInput").ap()
    wada_d = nc.dram_tensor("wada", [L * N_ADA, 128, 512], F32, kind="ExternalInput").ap()
    wp_d = nc.dram_tensor("wpack", [L * NU, 128, 1024], F32, kind="ExternalInput").ap()
    y_d = nc.dram_tensor("y", [T, D], F32, kind="ExternalOutput").ap()
    wbf_d = nc.dram_tensor("wbf", [L * NU, 128, 1024], BF16, kind="Internal").ap()
    b_wbf = [Buf("wbf%d" % i) for i in range(L * NU)]
    dbg_d = None
    if dbg:
        dbg_d = nc.dram_tensor("dbg", [128, 12, TT], F32, kind="ExternalOutput").ap()

    st = contextlib.ExitStack()
    with st:
        S = Sched(nc)

        def sbt(name, shape, dt=F32):
            return st.enter_context(nc.sbuf_tensor("sb_" + name, shape, dt))

        NBIG, NSM, NRP, NWS, NRAW, NST = 31, 40, 14, 6, 3, 4
        xT = sbt("xT", [128, 8, TT]); b_xT = [Buf("xT%d" % c) for c in range(8)]
        hT = sbt("hT", [128, 8, TT], BF16); b_hT = [Buf("hT%d" % c) for c in range(8)]
        yT = sbt("yT", [128, 12, TT], BF16); b_yT = [Buf("yT%d" % c) for c in range(12)]
        mT = sbt("mT", [128, 8, TT], BF16); b_mT = [Buf("mT%d" % c) for c in range(8)]
        wring = sbt("wring", [128, NWS, 1024], BF16); b_wr = [Buf("wr%d" % i) for i in range(NWS)]
        wstage = sbt("wstage", [128, NST, 1024]); b_ws = [Buf("ws%d" % i) for i in range(NST)]
        bigt = sbt("bigt", [128, NBIG, TT]); BP = SlotPool(bigt, NBIG, "big")
        smt = sbt("smt", [128, NSM, 128]); SP = SlotPool(smt, NSM, "sm")
        rpt = sbt("rpt", [128, NRP, 256]); RP = SlotPool(rpt, NRP, "rp")
        rawt = sbt("rawt", [128, NRAW, TT + 3]); RAWP = SlotPool(rawt, NRAW, "raw")
        lxb = sbt("lxb", [128, 4, TT], BF16); b_lxb = [Buf("lxb%d" % j) for j in range(4)]
        lbdt = sbt("lbdt", [128, 1024], BF16); b_lbd = Buf("lbd")
        cst = sbt("cst", [128, NCST]); b_cst = Buf("cst")
        pv = sbt("pv", [128, L, NPV]); b_pv = Buf("pv")
        cTt = sbt("cTt", [128, 8]); b_cT = Buf("cT")
        modt = sbt("modt", [128, L, 64]); b_mod = Buf("mod")
        Sg = sbt("Sg", [128, L * 4, 128]); b_Sg = [Buf("Sg%d" % i) for i in range(L * 4)]
        Ss = sbt("Ss", [128, L, 512]); b_Ss = [Buf("Ss%d" % i) for i in range(L)]
        hl = sbt("hl", [128, L * 4, 2]); b_hl = [Buf("hl%d" % i) for i in range(L * 4)]
        tails = sbt("tails", [128, L * 24, 3]); b_tl = [Buf("tl%d" % i) for i in range(L * 24)]
        pst = [st.enter_context(nc.psum_tensor("ps%d" % i, [128, 512], F32)) for i in range(8)]

        class PSPool:
            def __init__(self):
                self.free = list(range(8))
                self.bufs = [Buf("ps%d" % i, excl=True) for i in range(8)]

            def alloc(self):
                if not self.free:
                    raise RuntimeError("PSUM exhausted")
                i = self.free.pop(0)
                s = Slot()
                s.pool, s.i, s.ap, s.buf = self, i, pst[i][:], self.bufs[i]
                return s

            def release(self, s):
                assert s.i not in self.free
                self.free.append(s.i)

        PS = PSPool()

        ident = cst[:, C_ID:C_ID + 128]
        ones = cst[:, C_ONES:C_ONES + 128]
        NEGUS = cst[:, C_NEGUS:C_NEGUS + 128]
        NEGUI = cst[:, C_NEGUI:C_NEGUI + 128]
        POSLS = cst[:, C_POSLS:C_POSLS + 128]

        def sel(row):
            k = SELROWS.index(row)
            return cst[:, C_SEL + k * 128:C_SEL + (k + 1) * 128]

        def bl(x):
            return [y if isinstance(y, Buf) else y.buf for y in x]

        def PE_mm(out, lhsT, rhs, start, stop, reads, writes):
            S.op("pe", lambda e: e.matmul(out=out, lhsT=lhsT, rhs=rhs, start=start, stop=stop), bl(reads), bl(writes), noinc=(not stop))

        def PE_tr(out, in_, idn, reads, writes):
            S.op("pe", lambda e: e.transpose(out=out, in_=in_, identity=idn), bl(reads) + [b_cst], bl(writes))

        def ACT(out, in_, func, reads, writes, **kw):
            S.op("act", lambda e: e.activation(out=out, in_=in_, func=func, **kw), bl(reads), bl(writes))

        def V(method, reads, writes, **kw):
            S.op("dve", lambda e: getattr(e, method)(**kw), bl(reads), bl(writes))

        def DMA(q, out, in_, reads, writes):
            S.op(q, lambda e: e.dma_start(out=out, in_=in_), bl(reads), bl(writes), dma=True)

        def pvc(l, off, n=1):
            return pv[:, l, off:off + n]

        class WStream:
            def __init__(self):
                self.recorded = []
                self.seq = None if order is None else [(ti, l, u) for ti in range(NT) for l in range(L) for u in order]
                self.issued = 0
                self.used = 0

            @staticmethod
            def width(u):
                if 38 <= u < 86 and (u - 38) % 2 == 0:
                    return 512
                return 1024

            def _issue(self, i, ti, l, u):
                s = i % NWS
                w = self.width(u)
                g = l * NU + u
                if ti == 0:
                    ss = i % NST
                    DMA("sp", wstage[:, ss, 0:w], wp_d[g, :, 0:w], [], [b_ws[ss]])
                    ceng = ("pool", "dve", "act")[i % 3]
                    if ceng == "act":
                        S.op("act", lambda e, s=s, ss=ss, w=w: e.activation(out=wring[:, s, 0:w], in_=wstage[:, ss, 0:w], func=AF.Copy), [b_ws[ss]], [b_wr[s]])
                    else:
                        S.op(ceng, lambda e, s=s, ss=ss, w=w: e.tensor_copy(out=wring[:, s, 0:w], in_=wstage[:, ss, 0:w]), [b_ws[ss]], [b_wr[s]])
                    if NT > 1:
                        DMA("sp", wbf_d[g, :, 0:w], wring[:, s, 0:w], [b_wr[s]], [b_wbf[g]])
                else:
                    DMA("sp", wring[:, s, 0:w], wbf_d[g, :, 0:w], [b_wbf[g]], [b_wr[s]])

            def get(self, l, u):
                i = self.used
                self.used += 1
                if self.seq is None:
                    self.recorded.append(u)
                    self._issue(i, 0, l, u)
                else:
                    assert self.seq[i][1:] == (l, u), (self.seq[i], l, u)
                    lim = min(i + NWS - 1, len(self.seq) - 1)
                    while self.issued <= lim:
                        self._issue(self.issued, *self.seq[self.issued])
                        self.issued += 1
                s = i % NWS
                return wring[:, s, :], b_wr[s]

        W = WStream()

        DMA("sp", cst[:], cst_d, [], [b_cst])
        DMA("sp", pv[:], pv_d.rearrange("l p n -> p l n"), [], [b_pv])
        DMA("sp", cTt[:], cT_d, [], [b_cT])
        V("memset", [], b_Sg, ap=Sg[:], constant=0.0)
        V("memset", [], b_Ss, ap=Ss[:], constant=0.0)
        V("memset", [], b_hl, ap=hl[:], constant=0.0)
        V("memset", [], b_tl, ap=tails[:], constant=0.0)
        V("memset", [], [b_mod], ap=modt[:], constant=0.0)
        ACT(cTt[:], cTt[:], AF.Silu, [b_cT], [b_cT])
        for l in range(L):
            pm = PS.alloc()
            for ob in range(48):
                for half in range(2):
                    ws = BP.alloc()
                    DMA("sp", ws.ap, wada_d[(l * 48 + ob) * 2 + half], [], [ws])
                    for cc in range(4):
                        c = half * 4 + cc
                        PE_mm(pm.ap[:, ob:ob + 1], ws.ap[:, cc * 128:(cc + 1) * 128], cTt[:, c:c + 1],
                              c == 0, c == 7, [ws, b_cT], [pm])
                    BP.release(ws)
            mo = SP.alloc()
            V("tensor_tensor", [pm, b_pv], [mo], out=mo.ap[:, 0:48], in0=pm.ap[:, 0:48], in1=pvc(l, PV_ADAB, 48), op=ALU.add)
            PS.release(pm)
            m = modt[:, l, :]
            V("scalar_tensor_tensor", [mo, b_pv], [b_mod], out=m[:, 0:8], in0=mo.ap[:, 8:16], scalar=1.0, in1=pvc(l, PV_NM, 8), op0=ALU.add, op1=ALU.mult)
            V("tensor_copy", [mo], [b_mod], out=m[:, 8:16], in_=mo.ap[:, 0:8])
            V("tensor_copy", [mo], [b_mod], out=m[:, 16:24], in_=mo.ap[:, 16:24])
            V("scalar_tensor_tensor", [mo, b_pv], [b_mod], out=m[:, 24:32], in0=mo.ap[:, 32:40], scalar=1.0, in1=pvc(l, PV_NMLP, 8), op0=ALU.add, op1=ALU.mult)
            V("tensor_copy", [mo], [b_mod], out=m[:, 32:40], in_=mo.ap[:, 24:32])
            V("tensor_copy", [mo], [b_mod], out=m[:, 40:48], in_=mo.ap[:, 40:48])
            SP.release(mo)
            ACT(m[:, 48:49], pvc(l, PV_RA), AF.Exp, [b_pv], [b_mod])
            V("tensor_scalar", [b_mod], [b_mod], out=m[:, 48:49], in0=m[:, 48:49], scalar1=-1.0, scalar2=None, op0=ALU.mult)
            ACT(m[:, 52:56], pvc(l, PV_LAM, 4), AF.Exp, [b_pv], [b_mod], scale=-1.0)
            ACT(m[:, 52:56], m[:, 52:56], AF.Ln, [b_mod], [b_mod], bias=1.0)
            V("tensor_scalar", [b_mod], [b_mod], out=m[:, 52:56], in0=m[:, 52:56], scalar1=-8.0, scalar2=None, op0=ALU.mult)

        def MOD(l, k):
            return modt[:, l, k * 8:(k + 1) * 8]

        def rstd_bc(scale, eps, srcs, src_bufs):
            pm = PS.alloc()
            n = len(srcs)
            for i, (a, b) in enumerate(zip(srcs, src_bufs)):
                sq = BP.alloc()
                ACT(sq.ap, a, AF.Square, [b], [sq])
                PE_mm(pm.ap, ones, sq.ap, i == 0, i == n - 1, [sq, b_cst], [pm])
                BP.release(sq)
            r = BP.alloc()
            ACT(r.ap, pm.ap, AF.Ln, [pm], [r], scale=scale, bias=eps)
            PS.release(pm)
            ACT(r.ap, r.ap, AF.Exp, [r], [r], scale=-0.5)
            return r

        def norm_to_hT(l, kA, kB):
            r = rstd_bc(1.0 / D, EPS, [xT[:, c, :] for c in range(8)], b_xT)
            A, Bc = MOD(l, kA), MOD(l, kB)
            for c in range(8):
                t = BP.alloc()
                V("tensor_tensor", [b_xT[c], r], [t], out=t.ap, in0=xT[:, c, :], in1=r.ap, op=ALU.mult)
                ACT(hT[:, c, :], t.ap, AF.Identity, [t, b_mod], [b_hT[c]], scale=A[:, c:c + 1], bias=Bc[:, c:c + 1])
                BP.release(t)
            BP.release(r)

        def proj(l, u, rhs_fn, rhs_bufs, nk=8):
            wap, wb = W.get(l, u)
            pm = PS.alloc()
            for c in range(nk):
                PE_mm(pm.ap, wap[:, c * 128:(c + 1) * 128], rhs_fn(c), c == 0, c == nk - 1, [wb, rhs_bufs[c]], [pm])
            return pm

        def inproj(l, u):
            return proj(l, u, lambda c: hT[:, c, :], b_hT)

        def conv(l, pm, tblk, wcol, bias, func, out):
            raw = RAWP.alloc()
            tb = b_tl[l * 24 + tblk]
            tl = tails[:, l * 24 + tblk, :]
            ACT(raw.ap[:, 3:TT + 3], pm.ap, AF.Copy, [pm], [raw])
            PS.release(pm)
            V("tensor_copy", [tb], [raw], out=raw.ap[:, 0:3], in_=tl)
            V("tensor_copy", [raw], [tb], out=tl, in_=raw.ap[:, TT:TT + 3])
            V("tensor_scalar", [raw, b_pv], [out], out=out.ap, in0=raw.ap[:, 0:TT], scalar1=wcol[:, 0:1], scalar2=None, op0=ALU.mult)
            for k in range(1, 4):
                V("scalar_tensor_tensor", [raw, b_pv, out], [out], out=out.ap, in0=raw.ap[:, k:TT + k], scalar=wcol[:, k:k + 1],
                  in1=out.ap, op0=ALU.mult, op1=ALU.add)
            RAWP.release(raw)
            if bias is None:
                ACT(out.ap, out.ap, func, [out], [out])
            else:
                ACT(out.ap, out.ap, func, [out, b_pv], [out], bias=bias)

        gstop = [int(_os.environ.get("GSTOP", "100000"))]

        def interleave(gens):
            gens = list(gens)
            while gens:
                for g in list(gens):
                    if gstop[0] <= 0:
                        raise _Stop()
                    gstop[0] -= 1
                    try:
                        next(g)
                    except StopIteration:
                        gens.remove(g)

        dbg_state = {"done": False}

        def layer(l):
            if stop_after <= 1:
                raise _Stop()
            norm_to_hT(l, 0, 1)
            m = modt[:, l, :]
            negA = m[:, 48:49]
            GC = BP.alloc(); EE = BP.alloc()
            TM = []

            def small_gen():
                pm = inproj(l, 0)
                RB = BP.alloc(); KD = BP.alloc(); CB = BP.alloc()
                for t_ in (RB, GC, EE, KD, CB):
                    V("memset", [], [t_], ap=t_.ap, constant=0.0)
                ACT(RB.ap[0:32, :], pm.ap[0:32, :], AF.Sigmoid, [pm], [RB])
                sl = slice(32, 64)
                ACT(CB.ap[sl, :], pm.ap[sl, :], AF.Exp, [pm, b_pv], [CB], bias=pv[sl, l, PV_RB:PV_RB + 1])
                PS.release(pm)
                yield
                ACT(CB.ap[sl, :], CB.ap[sl, :], AF.Ln, [CB], [CB], bias=1.0)
                V("tensor_scalar", [CB, b_mod], [KD], out=KD.ap[sl, :], in0=CB.ap[sl, :], scalar1=negA[sl, :], scalar2=None, op0=ALU.mult)
                for ci in range(4):
                    cs = slice(ci * 128, (ci + 1) * 128)
                    V("tensor_tensor_scan", [KD, b_cst], [GC], out=GC.ap[sl, cs], data0=ones[sl, :], data1=KD.ap[sl, cs],
                      initial=0.0, op0=ALU.mult, op1=ALU.add)
                yield
                ACT(EE.ap[sl, :], GC.ap[sl, :], AF.Exp, [GC], [EE])
                for ci in range(4):
                    cs = slice(ci * 128, (ci + 1) * 128)
                    ACT(KD.ap[sl, cs], GC.ap[sl, cs], AF.Exp, [GC], [KD], scale=-1.0, bias=GC.ap[sl, ci * 128 + 127:ci * 128 + 128])
                V("tensor_copy", [EE], [CB], out=CB.ap[32:36, :], in_=EE.ap[32:36, :])
                yield
                for ci in range(4):
                    cs = slice(ci * 128, (ci + 1) * 128)
                    pt = PS.alloc()
                    for k, src in enumerate((RB, CB, KD, GC)):
                        PE_tr(pt.ap[:, k * 128:(k + 1) * 128], src.ap[:, cs], ident, [src], [pt])
                    tm = RP.alloc()
                    ACT(tm.ap[:, 0:192].rearrange("p (b c) -> p b c", b=4), pt.ap.rearrange("p (b c) -> p b c", b=4)[:, :, 0:48], AF.Copy, [pt], [tm])
                    PS.release(pt)
                    TM.append(tm)
                    if ci == 1:
                        yield
                BP.release(RB); BP.release(CB); BP.release(KD)


            if stop_after <= 2:
                raise _Stop()
            def gdn_head(h):
                hs = l * 4 + h
                qc = BP.alloc(); kc = BP.alloc(); vc = BP.alloc()
                ub, ust = 1 + 4 * h, 1
                pmq = inproj(l, ub)
                conv(l, pmq, h, pv[:, l, PV_GCW + 4 * h:PV_GCW + 4 * h + 4], None, AF.Silu, qc)
                yield
                pmk = inproj(l, ub + ust)
                conv(l, pmk, 4 + h, pv[:, l, PV_GCW + 4 * (4 + h):PV_GCW + 4 * (4 + h) + 4], None, AF.Silu, kc)
                yield
                pmv = inproj(l, ub + 2 * ust)
                conv(l, pmv, 8 + h, pv[:, l, PV_GCW + 4 * (8 + h):PV_GCW + 4 * (8 + h) + 4], None, AF.Silu, vc)
                yield
                rq = rstd_bc(1.0, EPS, [qc.ap], [qc.buf])
                V("scalar_tensor_tensor", [qc, rq], [qc], out=qc.ap, in0=qc.ap, scalar=128.0 ** -0.5, in1=rq.ap, op0=ALU.mult, op1=ALU.mult)
                BP.release(rq)
                yield
                rk = rstd_bc(1.0, EPS, [kc.ap], [kc.buf])
                V("tensor_tensor", [kc, rk], [kc], out=kc.ap, in0=kc.ap, in1=rk.ap, op=ALU.mult)
                BP.release(rk)
                yield
                pe_ = PS.alloc()
                PE_mm(pe_.ap, sel(32 + h), EE.ap, True, True, [EE, b_cst], [pe_])
                qd = BP.alloc()
                if _os.environ.get("SKIPQD") is None:
                    V("tensor_tensor", [qc, pe_], [qd], out=qd.ap, in0=qc.ap, in1=pe_.ap, op=ALU.mult)
                gts = SP.alloc()
                V("tensor_copy", [pe_], [gts], out=gts.ap[:, 0:4], in_=pe_.ap.rearrange("p (c t) -> p c t", c=4)[:, :, 127])
                PS.release(pe_)
                oT = BP.alloc()
                yield
                for ci in range(4):
                    cs = slice(ci * 128, (ci + 1) * 128)
                    tm = TM[ci]
                    beta = tm.ap[:, h:h + 1]
                    Eg = tm.ap[:, 48 + 32 + h:48 + 32 + h + 1]
                    KDc = tm.ap[:, 96 + 32 + h:96 + 32 + h + 1]
                    gc_ = tm.ap[:, 144 + 32 + h:144 + 32 + h + 1]
                    pt = PS.alloc()
                    PE_tr(pt.ap[:, 0:128], kc.ap[:, cs], ident, [kc], [pt])
                    PE_tr(pt.ap[:, 128:256], vc.ap[:, cs], ident, [vc], [pt])
                    kb = SP.alloc(); kbg = SP.alloc(); kd = SP.alloc(); bv = SP.alloc()
                    V("tensor_scalar", [pt, tm], [kb], out=kb.ap, in0=pt.ap[:, 0:128], scalar1=beta, scalar2=None, op0=ALU.mult)
                    V("tensor_scalar", [kb, tm], [kbg], out=kbg.ap, in0=kb.ap, scalar1=Eg, scalar2=None, op0=ALU.mult)
                    ACT(kd.ap, pt.ap[:, 0:128], AF.Identity, [pt, tm], [kd], scale=KDc)
                    ACT(bv.ap, pt.ap[:, 128:256], AF.Identity, [pt, tm], [bv], scale=beta)
                    PE_tr(pt.ap[:, 256:384], kb.ap, ident, [kb], [pt])
                    kbT = SP.alloc()
                    ACT(kbT.ap, pt.ap[:, 256:384], AF.Copy, [pt], [kbT])
                    PS.release(pt)
                    SP.release(kb)
                    yield
                    pr = PS.alloc()
                    PE_mm(pr.ap[:, 0:128], kc.ap[:, cs], kbT.ap, True, True, [kc, kbT], [pr])
                    PE_mm(pr.ap[:, 128:256], kbT.ap, kc.ap[:, cs], True, True, [kc, kbT], [pr])
                    PE_mm(pr.ap[:, 256:384], kc.ap[:, cs], qc.ap[:, cs], True, True, [kc, qc], [pr])
                    PE_mm(pr.ap[:, 384:512], sel(32 + h), GC.ap[:, cs], True, True, [GC, b_cst], [pr])
                    SP.release(kbT)
                    dts = SP.alloc(); dl = SP.alloc()
                    V("scalar_tensor_tensor", [pr, tm, b_cst], [dts], out=dts.ap, in0=pr.ap[:, 384:512], scalar=gc_, in1=NEGUS, op0=ALU.subtract, op1=ALU.add)
                    V("scalar_tensor_tensor", [pr, tm, b_cst], [dl], out=dl.ap, in0=pr.ap[:, 384:512], scalar=gc_, in1=POSLS, op0=ALU.subtract, op1=ALU.add)
                    ACT(dts.ap, dts.ap, AF.Exp, [dts], [dts])
                    ACT(dl.ap, dl.ap, AF.Exp, [dl], [dl], scale=-1.0)
                    R = RP.alloc()
                    V("scalar_tensor_tensor", [pr, dts], [R], out=R.ap[:, 0:128], in0=pr.ap[:, 0:128], scalar=-1.0, in1=dts.ap, op0=ALU.mult, op1=ALU.mult)
                    V("tensor_copy", [b_cst], [R], out=R.ap[:, 128:256], in_=ident)
                    A = SP.alloc()
                    V("scalar_tensor_tensor", [pr, dl], [A], out=A.ap, in0=pr.ap[:, 128:256], scalar=-1.0, in1=dl.ap, op0=ALU.mult, op1=ALU.mult)
                    SP.release(dl)
                    V("tensor_tensor", [dts, b_cst], [dts], out=dts.ap, in0=dts.ap, in1=ident, op=ALU.add)
                    qkT = dts
                    V("tensor_tensor", [pr, qkT], [qkT], out=qkT.ap, in0=pr.ap[:, 256:384], in1=qkT.ap, op=ALU.mult)
                    PS.release(pr)
                    yield
                    for j in range(7):
                        pn = PS.alloc()
                        if j < 6:
                            PE_mm(pn.ap[:, 0:256], A.ap, R.ap, True, True, [A, R], [pn])
                            PE_mm(pn.ap[:, 256:384], R.ap[:, 0:128], A.ap, True, True, [A, R], [pn])
                            R2 = RP.alloc(); A2 = SP.alloc()
                            ACT(R2.ap[:, 0:128], pn.ap[:, 0:128], AF.Copy, [pn], [R2])
                            V("tensor_tensor", [pn, R], [R2], out=R2.ap[:, 128:256], in0=pn.ap[:, 128:256], in1=R.ap[:, 128:256], op=ALU.add)
                            ACT(A2.ap, pn.ap[:, 256:384], AF.Copy, [pn], [A2])
                            RP.release(R); SP.release(A)
                            R, A = R2, A2
                        else:
                            PE_mm(pn.ap[:, 0:128], A.ap, R.ap[:, 128:256], True, True, [A, R], [pn])
                            V("tensor_tensor", [pn, R], [R], out=R.ap[:, 128:256], in0=pn.ap[:, 0:128], in1=R.ap[:, 128:256], op=ALU.add)
                            SP.release(A)
                        PS.release(pn)
                        yield
                    Q = R.ap[:, 128:256]
                    pw = PS.alloc()
                    PE_mm(pw.ap[:, 0:128], kbg.ap, Q, True, True, [kbg, R], [pw])
                    nwT = SP.alloc()
                    ACT(nwT.ap, pw.ap[:, 0:128], AF.Copy, [pw], [nwT], scale=-1.0)
                    SP.release(kbg)
                    yield
                    Sh = Sg[:, hs, :]
                    PE_mm(pw.ap[:, 128:256], Q, bv.ap, True, False, [R, bv], [pw])
                    PE_mm(pw.ap[:, 128:256], nwT.ap, Sh, False, True, [nwT, b_Sg[hs]], [pw])
                    vn = SP.alloc()
                    ACT(vn.ap, pw.ap[:, 128:256], AF.Copy, [pw], [vn])
                    RP.release(R); SP.release(bv); SP.release(nwT)
                    yield
                    PE_mm(pw.ap[:, 256:384], Sh, qd.ap[:, cs], True, False, [b_Sg[hs], qd], [pw])
                    PE_mm(pw.ap[:, 256:384], vn.ap, qkT.ap, False, True, [vn, qkT], [pw])
                    PE_mm(pw.ap[:, 384:512], kd.ap, vn.ap, True, True, [kd, vn], [pw])
                    ACT(oT.ap[:, cs], pw.ap[:, 256:384], AF.Copy, [pw], [oT])
                    V("scalar_tensor_tensor", [b_Sg[hs], gts, pw], [b_Sg[hs]], out=Sh, in0=Sh, scalar=gts.ap[:, ci:ci + 1], in1=pw.ap[:, 384:512],
                      op0=ALU.mult, op1=ALU.add)
                    PS.release(pw)
                    SP.release(vn); SP.release(kd); SP.release(qkT)
                    yield
                SP.release(gts)
                BP.release(qd); BP.release(qc); BP.release(kc); BP.release(vc)
                pmz = inproj(l, ub + 3 * ust)
                sz = BP.alloc()
                ACT(sz.ap, pmz.ap, AF.Silu, [pmz], [sz])
                PS.release(pmz)
                yield
                r = rstd_bc(1.0 / 128, EPS, [oT.ap], [oT.buf])
                V("tensor_tensor", [oT, r], [oT], out=oT.ap, in0=oT.ap, in1=r.ap, op=ALU.mult)
                BP.release(r)
                V("scalar_tensor_tensor", [oT, b_pv, sz], [b_yT[h]], out=yT[:, h, :], in0=oT.ap, scalar=pvc(l, PV_GNW), in1=sz.ap, op0=ALU.mult, op1=ALU.mult)
                BP.release(oT); BP.release(sz)
                yield

            GDN_PHASE_MARK = None

            if stop_after <= 3:
                raise _Stop()
            def ssd_gen():
                sx = [BP.alloc() for _ in range(4)]
                sbm = [BP.alloc() for _ in range(2)]
                scm = [BP.alloc() for _ in range(2)]
                for j in range(4):
                    pm = inproj(l, 17 + j)
                    conv(l, pm, 12 + j, pv[:, l, PV_SCW + 4 * j:PV_SCW + 4 * j + 4], pvc(l, PV_SCB + j), AF.Silu, sx[j])
                    yield
                for g in range(2):
                    pm = inproj(l, 21 + g)
                    conv(l, pm, 16 + g, pv[:, l, PV_SCW + 4 * (4 + g):PV_SCW + 4 * (4 + g) + 4], pvc(l, PV_SCB + 4 + g), AF.Silu, sbm[g])
                    yield
                for g in range(2):
                    pm = inproj(l, 23 + g)
                    conv(l, pm, 18 + g, pv[:, l, PV_SCW + 4 * (6 + g):PV_SCW + 4 * (6 + g) + 4], pvc(l, PV_SCB + 6 + g), AF.Silu, scm[g])
                    yield
                yb = [BP.alloc() for _ in range(4)]
                Sst = Ss[:, l, :]

                def chunk_pre(ci, res):
                    cs = slice(ci * 128, (ci + 1) * 128)
                    tm = TM[ci]
                    dt8 = tm.ap[:, 48 + 40:48 + 48]
                    kd8 = tm.ap[:, 96 + 40:96 + 48]
                    pt = PS.alloc()
                    for j in range(4):
                        PE_tr(pt.ap[:, j * 128:(j + 1) * 128], sx[j].ap[:, cs], ident, [sx[j]], [pt])
                    xdt = BP.alloc(); xdd = BP.alloc()
                    V("tensor_tensor", [pt, tm], [xdt], out=xdt.ap.rearrange("p (h q) -> p h q", h=8), in0=pt.ap.rearrange("p (h q) -> p h q", h=8),
                      in1=dt8.unsqueeze(2).to_broadcast([128, 8, 64]), op=ALU.mult)
                    PS.release(pt)
                    V("tensor_tensor", [xdt, tm], [xdd], out=xdd.ap.rearrange("p (h q) -> p h q", h=8), in0=xdt.ap.rearrange("p (h q) -> p h q", h=8),
                      in1=kd8.unsqueeze(2).to_broadcast([128, 8, 64]), op=ALU.mult)
                    yield
                    pt = PS.alloc()
                    for g in range(2):
                        PE_tr(pt.ap[:, g * 128:(g + 1) * 128], sbm[g].ap[:, cs], ident, [sbm[g]], [pt])
                        PE_mm(pt.ap[:, 256 + g * 128:256 + (g + 1) * 128], sbm[g].ap[:, cs], scm[g].ap[:, cs], True, True, [sbm[g], scm[g]], [pt])
                    bmt = RP.alloc()
                    ACT(bmt.ap, pt.ap[:, 0:256], AF.Copy, [pt], [bmt])
                    cbm = RP.alloc()
                    ACT(cbm.ap, pt.ap[:, 256:512], AF.Copy, [pt], [cbm])
                    PS.release(pt)
                    yield
                    Wl = [BP.alloc(), BP.alloc()]
                    Eb = [BP.alloc(), BP.alloc()]
                    cdec = SP.alloc()
                    for b2 in range(2):
                        pa = PS.alloc()
                        for h4 in range(4):
                            PE_mm(pa.ap[:, h4 * 128:(h4 + 1) * 128], sel(40 + b2 * 4 + h4), GC.ap[:, cs], True, True, [GC, b_cst], [pa])
                        for h4 in range(4):
                            hh, o = b2 * 4 + h4, h4 * 128
                            V("scalar_tensor_tensor", [pa, tm, b_cst], [Wl[b2]], out=Wl[b2].ap[:, o:o + 128], in0=pa.ap[:, o:o + 128],
                              scalar=tm.ap[:, 144 + 40 + hh:144 + 40 + hh + 1], in1=NEGUI, op0=ALU.subtract, op1=ALU.add)
                        ACT(Eb[b2].ap, pa.ap, AF.Exp, [pa], [Eb[b2]])
                        PS.release(pa)
                        ACT(Wl[b2].ap, Wl[b2].ap, AF.Exp, [Wl[b2]], [Wl[b2]])
                        yield
                        V("tensor_tensor", [Wl[b2], cbm], [Wl[b2]], out=Wl[b2].ap.rearrange("p (h c) -> p h c", h=4),
                          in0=Wl[b2].ap.rearrange("p (h c) -> p h c", h=4),
                          in1=cbm.ap[:, b2 * 128:(b2 + 1) * 128].unsqueeze(1).to_broadcast([128, 4, 128]), op=ALU.mult)
                        V("tensor_copy", [Eb[b2]], [cdec], out=cdec.ap[:, b2 * 4:(b2 + 1) * 4],
                          in_=Eb[b2].ap.rearrange("p (h c) -> p h c", h=4)[:, :, 127])
                        V("tensor_tensor", [Eb[b2], scm[b2]], [Eb[b2]], out=Eb[b2].ap.rearrange("p (h c) -> p h c", h=4),
                          in0=Eb[b2].ap.rearrange("p (h c) -> p h c", h=4),
                          in1=scm[b2].ap[:, cs].unsqueeze(1).to_broadcast([128, 4, 128]), op=ALU.mult)
                        yield
                    RP.release(cbm)
                    res[ci] = (xdt, xdd, bmt, Wl, Eb, cdec)

                def chunk_fin(ci, res):
                    cs = slice(ci * 128, (ci + 1) * 128)
                    xdt, xdd, bmt, Wl, Eb, cdec = res[ci]
                    py = PS.alloc()
                    for hh in range(8):
                        b2, o = hh // 4, (hh % 4) * 128
                        blk, half = hh // 2, hh % 2
                        outp = py.ap[half * 64:(half + 1) * 64, blk * 128:(blk + 1) * 128]
                        PE_mm(outp, xdt.ap[:, hh * 64:(hh + 1) * 64], Wl[b2].ap[:, o:o + 128], True, False, [xdt, Wl[b2]], [py])
                        PE_mm(outp, Sst[:, hh * 64:(hh + 1) * 64], Eb[b2].ap[:, o:o + 128], False, True, [b_Ss[l], Eb[b2]], [py])
                    pu = PS.alloc()
                    for g in range(2):
                        PE_mm(pu.ap[:, g * 256:(g + 1) * 256], bmt.ap[:, g * 128:(g + 1) * 128], xdd.ap[:, g * 256:(g + 1) * 256], True, True, [bmt, xdd], [pu])
                    for blk in range(4):
                        V("scalar_tensor_tensor", [sx[blk], b_pv, py], [yb[blk]], out=yb[blk].ap[:, cs], in0=sx[blk].ap[:, cs],
                          scalar=pvc(l, PV_SD + blk), in1=py.ap[:, blk * 128:(blk + 1) * 128], op0=ALU.mult, op1=ALU.add)
                    PS.release(py)
                    V("tensor_tensor", [b_Ss[l], cdec], [b_Ss[l]], out=Sst.rearrange("p (h q) -> p h q", h=8), in0=Sst.rearrange("p (h q) -> p h q", h=8),
                      in1=cdec.ap[:, 0:8].unsqueeze(2).to_broadcast([128, 8, 64]), op=ALU.mult)
                    V("tensor_tensor", [b_Ss[l], pu], [b_Ss[l]], out=Sst, in0=Sst, in1=pu.ap, op=ALU.add)
                    PS.release(pu)
                    BP.release(Wl[0]); BP.release(Wl[1]); BP.release(Eb[0]); BP.release(Eb[1]); BP.release(xdt)
                    RP.release(bmt); SP.release(cdec); BP.release(xdd)

                res = {}
                for pair in ((0, 1), (2, 3)):
                    gens = [chunk_pre(ci, res) for ci in pair]
                    while gens:
                        for g_ in list(gens):
                            try:
                                next(g_)
                            except StopIteration:
                                gens.remove(g_)
                        yield
                    for ci in pair:
                        chunk_fin(ci, res)
                        yield
                for s_ in sx + sbm + scm:
                    BP.release(s_)
                BP.release(GC); BP.release(EE)
                for tm in TM:
                    RP.release(tm)
                for j in range(4):
                    pm = inproj(l, 25 + j)
                    sz = BP.alloc()
                    ACT(sz.ap, pm.ap, AF.Silu, [pm], [sz])
                    PS.release(pm)
                    V("tensor_tensor", [yb[j], sz], [yb[j]], out=yb[j].ap, in0=yb[j].ap, in1=sz.ap, op=ALU.mult)
                    BP.release(sz)
                    yield
                for g in range(2):
                    r = rstd_bc(1.0 / 256, EPS, [yb[2 * g].ap, yb[2 * g + 1].ap], [yb[2 * g].buf, yb[2 * g + 1].buf])
                    for j in (2 * g, 2 * g + 1):
                        V("scalar_tensor_tensor", [yb[j], b_pv, r], [b_yT[4 + j]], out=yT[:, 4 + j, :], in0=yb[j].ap, scalar=pvc(l, PV_SNW + j), in1=r.ap,
                          op0=ALU.mult, op1=ALU.mult)
                    BP.release(r)
                    yield
                for s_ in yb:
                    BP.release(s_)


            def lru_gen():
                wl_ap, wl_b = W.get(l, 37)
                S.op("pool", lambda e: e.tensor_copy(out=lbdt[:], in_=wl_ap), [wl_b], [b_lbd])
                lbd_ap, lbd_b = lbdt[:], b_lbd
                yield
                for j in range(4):
                    xc = BP.alloc()
                    pm = inproj(l, 29 + 2 * j)
                    conv(l, pm, 20 + j, pv[:, l, PV_LCW + 4 * j:PV_LCW + 4 * j + 4], pvc(l, PV_LCB + j), AF.Identity, xc)
                    V("tensor_copy", [xc], [b_lxb[j]], out=lxb[:, j, :], in_=xc.ap)
                    yield
                    pg = inproj(l, 30 + 2 * j)
                    gg = BP.alloc()
                    ACT(gg.ap, pg.ap, AF.Gelu_apprx_tanh, [pg], [gg])
                    PS.release(pg)
                    yield
                    pa_ = PS.alloc(); pi_ = PS.alloc()
                    PE_mm(pa_.ap, lbd_ap[:, j * 128:(j + 1) * 128], lxb[:, j, :], True, True, [lbd_b, b_lxb[j]], [pa_])
                    PE_mm(pi_.ap, lbd_ap[:, (4 + j) * 128:(5 + j) * 128], lxb[:, j, :], True, True, [lbd_b, b_lxb[j]], [pi_])
                    ra = BP.alloc(); ri = BP.alloc()
                    ACT(ra.ap, pa_.ap, AF.Sigmoid, [pa_, b_pv], [ra], bias=pvc(l, PV_LBA + j))
                    ACT(ri.ap, pi_.ap, AF.Sigmoid, [pi_, b_pv], [ri], bias=pvc(l, PV_LBX + j))
                    PS.release(pa_); PS.release(pi_)
                    yield
                    ACT(ra.ap, ra.ap, AF.Exp, [ra, b_mod], [ra], scale=m[:, 52 + j:53 + j])
                    mu = BP.alloc()
                    V("tensor_tensor", [ra], [mu], out=mu.ap, in0=ra.ap, in1=ra.ap, op=ALU.mult)
                    ACT(mu.ap, mu.ap, AF.Sqrt, [mu], [mu], scale=-1.0, bias=1.0)
                    V("tensor_tensor", [ri, xc], [ri], out=ri.ap, in0=ri.ap, in1=xc.ap, op=ALU.mult)
                    yield
                    V("tensor_tensor", [ri, mu], [mu], out=mu.ap, in0=ri.ap, in1=mu.ap, op=ALU.mult)
                    hb = b_hl[l * 4 + j]
                    V("tensor_tensor_scan", [ra, mu, hb], [xc], out=xc.ap, data0=ra.ap, data1=mu.ap, initial=hl[:, l * 4 + j, 1:2], op0=ALU.mult, op1=ALU.add)
                    V("tensor_copy", [xc], [hb], out=hl[:, l * 4 + j, :], in_=xc.ap[:, TT - 2:TT])
                    V("tensor_tensor", [xc, gg], [b_yT[8 + j]], out=yT[:, 8 + j, :], in0=xc.ap, in1=gg.ap, op=ALU.mult)
                    BP.release(ra); BP.release(ri); BP.release(mu); BP.release(xc); BP.release(gg)
                    yield

            interleave([small_gen(), gdn_head(0), gdn_head(1), gdn_head(2), gdn_head(3), lru_gen()])
            interleave([ssd_gen()])

            if dbg and not dbg_state["done"]:
                dbg_state["done"] = True
                for k in range(12):
                    t = BP.alloc()
                    V("tensor_copy", [b_yT[k]], [t], out=t.ap, in_=yT[:, k, :])
                    DMA("sp", dbg_d[:, k, :], t.ap, [t], [])
                    BP.release(t)

            if stop_after <= 5:
                raise _Stop()
            for ob in range(8):
                acc = BP.alloc()
                for r in range(3):
                    u = 38 + (ob * 3 + r) * 2
                    pb = proj(l, u, lambda c, r=r: yT[:, 4 * r + c, :], b_yT[4 * r:4 * r + 4], nk=4)
                    pg = inproj(l, u + 1)
                    sg = BP.alloc()
                    ACT(sg.ap, pg.ap, AF.Sigmoid, [pg], [sg])
                    PS.release(pg)
                    if r == 0:
                        V("tensor_tensor", [pb, sg], [acc], out=acc.ap, in0=pb.ap, in1=sg.ap, op=ALU.mult)
                    else:
                        V("tensor_tensor", [pb, sg], [sg], out=sg.ap, in0=pb.ap, in1=sg.ap, op=ALU.mult)
                        if r == 1:
                            V("tensor_tensor", [acc, sg], [acc], out=acc.ap, in0=acc.ap, in1=sg.ap, op=ALU.add)
                        else:
                            V("tensor_tensor", [acc, sg], [b_mT[ob]], out=mT[:, ob, :], in0=acc.ap, in1=sg.ap, op=ALU.add)
                    PS.release(pb)
                    BP.release(sg)
                BP.release(acc)
            G1 = MOD(l, 2)
            for ob in range(8):
                pm = proj(l, 86 + ob, lambda c: mT[:, c, :], b_mT)
                V("scalar_tensor_tensor", [pm, b_mod, b_xT[ob]], [b_xT[ob]], out=xT[:, ob, :], in0=pm.ap, scalar=G1[:, ob:ob + 1], in1=xT[:, ob, :],
                  op0=ALU.mult, op1=ALU.add)
                PS.release(pm)

            if stop_after <= 6:
                raise _Stop()
            norm_to_hT(l, 3, 4)
            asl = [BP.alloc() for _ in range(16)]
            aT = [s_.ap.bitcast(BF16).rearrange("p (a t) -> p a t", a=2) for s_ in asl]

            def aT_ap(c):
                return aT[c // 2][:, c % 2, :]

            for ob in range(32):
                pm = inproj(l, 94 + ob)
                sq = BP.alloc()
                ACT(sq.ap, pm.ap, AF.Relu, [pm], [sq])
                V("tensor_tensor", [sq, pm], [asl[ob // 2]], out=aT_ap(ob), in0=sq.ap, in1=pm.ap, op=ALU.mult)
                PS.release(pm)
                BP.release(sq)
            G2 = MOD(l, 5)
            for ob in range(8):
                pm = PS.alloc()
                for qtr in range(4):
                    wap, wb = W.get(l, 126 + ob * 4 + qtr)
                    for cc in range(8):
                        c = qtr * 8 + cc
                        PE_mm(pm.ap, wap[:, cc * 128:(cc + 1) * 128], aT_ap(c), c == 0, c == 31, [wb, asl[c // 2]], [pm])
                V("scalar_tensor_tensor", [pm, b_mod, b_xT[ob]], [b_xT[ob]], out=xT[:, ob, :], in0=pm.ap, scalar=G2[:, ob:ob + 1], in1=xT[:, ob, :],
                  op0=ALU.mult, op1=ALU.add)
                PS.release(pm)
            for s_ in asl:
                BP.release(s_)

        for ti in range(NT):
            xs = [BP.alloc() for _ in range(8)]
            for s_ in range(4):
                for hf in range(2):
                    DMA("sp", xs[2 * s_ + hf].ap, x_d[ti * TT + s_ * 128:ti * TT + (s_ + 1) * 128, hf * 512:(hf + 1) * 512], [], [xs[2 * s_ + hf]])
            for c in range(8):
                pt = PS.alloc()
                for s_ in range(4):
                    src = xs[2 * s_ + c // 4]
                    PE_tr(pt.ap[:, s_ * 128:(s_ + 1) * 128], src.ap[:, (c % 4) * 128:(c % 4 + 1) * 128], ident, [src], [pt])
                ACT(xT[:, c, :], pt.ap, AF.Copy, [pt], [b_xT[c]])
                PS.release(pt)
            for s_ in xs:
                BP.release(s_)
            try:
                for l in range(L):
                    layer(l)
            except _Stop:
                for P_ in (BP, SP, RP, RAWP, PS):
                    P_.free = list(range(len(P_.bufs)))
            r = rstd_bc(1.0 / D, EPS, [xT[:, c, :] for c in range(8)], b_xT)
            xo = [BP.alloc() for _ in range(8)]
            fn = [BP.alloc() for _ in range(8)]
            for c in range(8):
                V("scalar_tensor_tensor", [b_xT[c], b_pv, r], [fn[c]], out=fn[c].ap, in0=xT[:, c, :], scalar=pv[:, 0, PV_FNW + c:PV_FNW + c + 1], in1=r.ap,
                  op0=ALU.mult, op1=ALU.mult)
            BP.release(r)
            for s_ in range(4):
                for hf in range(2):
                    pt = PS.alloc()
                    for cc in range(4):
                        c = hf * 4 + cc
                        PE_tr(pt.ap[:, cc * 128:(cc + 1) * 128], fn[c].ap[:, s_ * 128:(s_ + 1) * 128], ident, [fn[c]], [pt])
                    dst = xo[2 * s_ + hf]
                    ACT(dst.ap, pt.ap, AF.Copy, [pt], [dst])
                    PS.release(pt)
                    DMA("sp", y_d[ti * TT + s_ * 128:ti * TT + (s_ + 1) * 128, hf * 512:(hf + 1) * 512], dst.ap, [dst], [])
            for s_ in xo + fn:
                BP.release(s_)
        S.final_wait("sp")
        build_program.recorded = W.recorded
        build_program.stats = dict(n_ops=S.n_ops, per_eng={e: len(S.ops[e]) for e in ENGS},
                                   minfree=dict(big=BP.minfree, sm=SP.minfree, rp=RP.minfree))
        S.emit()
    return nc


def _unit(Wm, cols, nk=8):
    K = Wm.shape[0]
    out = np.zeros((128, nk, 128), np.float32)
    cols = np.asarray(cols)
    ok = cols >= 0
    sub = Wm[:nk * 128][:, cols[ok]]
    out[:, :, ok] = sub.reshape(nk, 128, -1).transpose(1, 0, 2)
    return out.reshape(128, nk * 128)


def _colvec(v):
    v = np.asarray(v, np.float32)
    return np.ascontiguousarray(v.reshape(-1, 128).T)


def make_consts():
    c = np.zeros((128, NCST), np.float32)
    p = np.arange(128)[:, None]
    f = np.arange(128)[None, :]
    c[:, C_ID:C_ID + 128] = (p == f)
    c[:, C_ONES:C_ONES + 128] = 1.0
    c[:, C_NEGUS:C_NEGUS + 128] = np.where(f > p, 0.0, -1e30)
    c[:, C_NEGUI:C_NEGUI + 128] = np.where(f >= p, 0.0, -1e30)
    c[:, C_POSLS:C_POSLS + 128] = np.where(f < p, 0.0, 1e30)
    for k, r in enumerate(SELROWS):
        c[r, C_SEL + k * 128:C_SEL + (k + 1) * 128] = 1.0
    return c


def pack_layer_weights(w_in, w_branch, w_out, w_up, w_down, lru_w_a, lru_w_x):
    units = np.zeros((NU, 128, 1024), np.float32)
    ar = np.arange(128)
    small = -np.ones(128, np.int64)
    small[0:4] = 2048 + np.arange(4)
    small[32:36] = 2052 + np.arange(4)
    small[40:48] = 3592 + np.arange(8)
    units[0] = _unit(w_in, small)
    for h in range(4):
        ub, ust = 1 + 4 * h, 1
        units[ub] = _unit(w_in, 0 + 128 * h + ar)
        units[ub + ust] = _unit(w_in, 512 + 128 * h + ar)
        units[ub + 2 * ust] = _unit(w_in, 1024 + 128 * h + ar)
        units[ub + 3 * ust] = _unit(w_in, 1536 + 128 * h + ar)
    for j in range(4):
        units[17 + j] = _unit(w_in, 2056 + 128 * j + ar)
        units[25 + j] = _unit(w_in, 2568 + 128 * j + ar)
        units[29 + 2 * j] = _unit(w_in, 3600 + 128 * j + ar)
        units[30 + 2 * j] = _unit(w_in, 4112 + 128 * j + ar)
    for g in range(2):
        units[21 + g] = _unit(w_in, 3080 + 128 * g + ar)
        units[23 + g] = _unit(w_in, 3336 + 128 * g + ar)
    bd = np.zeros((128, 8, 128), np.float32)
    for j in range(4):
        for hb in range(2):
            blk = 2 * j + hb
            bd[hb * 64:(hb + 1) * 64, j, hb * 64:(hb + 1) * 64] = lru_w_a[blk]
            bd[hb * 64:(hb + 1) * 64, 4 + j, hb * 64:(hb + 1) * 64] = lru_w_x[blk]
    units[37] = bd.reshape(128, 1024)
    for ob in range(8):
        for r in range(3):
            u = 38 + (ob * 3 + r) * 2
            units[u, :, 0:512] = _unit(w_branch[r], ob * 128 + ar, nk=4)
            units[u + 1] = _unit(w_in, 4624 + r * 1024 + ob * 128 + ar)
        units[86 + ob] = _unit(w_out, ob * 128 + ar)
    for ob in range(32):
        units[94 + ob] = _unit(w_up, ob * 128 + ar)
    for ob in range(8):
        for qtr in range(4):
            units[126 + ob * 4 + qtr] = _unit(w_down[qtr * 1024:(qtr + 1) * 1024], ob * 128 + ar)
    return units


def pack_ada(ada_w_l):
    out = np.zeros((N_ADA, 128, 512), np.float32)
    ar = np.arange(128)
    for ob in range(48):
        for half in range(2):
            out[ob * 2 + half] = _unit(ada_w_l[half * 512:(half + 1) * 512], ob * 128 + ar, nk=4)
    return out


def pack_pv(l, p):
    v = np.zeros((128, NPV), np.float32)
    v[:, PV_NM:PV_NM + 8] = _colvec(p["norm_mix"][l])
    v[:, PV_NMLP:PV_NMLP + 8] = _colvec(p["norm_mlp"][l])
    v[:, PV_ADAB:PV_ADAB + 48] = _colvec(p["ada_b"][l])
    gcw = p["gdn_conv_w"][l]
    for b in range(12):
        v[:, PV_GCW + 4 * b:PV_GCW + 4 * b + 4] = gcw[:, b * 128:(b + 1) * 128].T
    scw = p["ssd_conv_w"][l]
    for b in range(8):
        v[:, PV_SCW + 4 * b:PV_SCW + 4 * b + 4] = scw[:, b * 128:(b + 1) * 128].T
    v[:, PV_SCB:PV_SCB + 8] = _colvec(p["ssd_conv_b"][l])
    lcw = p["lru_conv_w"][l]
    for b in range(4):
        v[:, PV_LCW + 4 * b:PV_LCW + 4 * b + 4] = lcw[:, b * 128:(b + 1) * 128].T
    v[:, PV_LCB:PV_LCB + 4] = _colvec(p["lru_conv_b"][l])
    v[:, PV_LBA:PV_LBA + 4] = _colvec(p["lru_b_a"][l])
    v[:, PV_LBX:PV_LBX + 4] = _colvec(p["lru_b_x"][l])
    v[:, PV_LAM:PV_LAM + 4] = _colvec(p["lru_lambda"][l])
    v[:, PV_GNW] = p["gdn_norm"][l]
    v[:, PV_SNW:PV_SNW + 4] = _colvec(p["ssd_norm"][l])
    v[:, PV_SD:PV_SD + 4] = _colvec(np.repeat(np.asarray(p["ssd_d"][l]), 64))
    v[32:36, PV_RB] = p["gdn_dt_bias"][l]
    v[40:48, PV_RB] = p["ssd_dt_bias"][l]
    v[32:36, PV_RA] = p["gdn_a_log"][l]
    v[40:48, PV_RA] = p["ssd_a_log"][l]
    v[:, PV_FNW:PV_FNW + 8] = _colvec(p["final_norm"])
    return v


def prepare_shared(p, L):
    p = {k: np.asarray(v, np.float32) for k, v in p.items()}
    wpack = np.concatenate([pack_layer_weights(p["w_in"][l], p["w_branch"][l], p["w_out"][l], p["w_up"][l], p["w_down"][l],
                                               p["lru_w_a"][l], p["lru_w_x"][l]) for l in range(L)], axis=0)
    wada = np.concatenate([pack_ada(p["ada_w"][l]) for l in range(L)], axis=0)
    pvv = np.stack([pack_pv(l, p) for l in range(L)], axis=0)
    return {"wpack": wpack, "wada": wada, "pv": pvv, "cst": make_consts()}


_ORDER = []


def unit_order():
    if not _ORDER:
        build_program(TT, 1, order=None)
        rec = list(build_program.recorded)
        assert sorted(rec) == list(range(NU)), len(rec)
        _ORDER.extend(rec)
    return list(_ORDER)


def kernel(**inputs):
    x = np.asarray(inputs["x"], np.float32)
    c = np.asarray(inputs["c"], np.float32)
    B, T, _ = x.shape
    L = inputs["w_in"].shape[0]
    params = {k: v for k, v in inputs.items() if k not in ("x", "c")}
    shared = prepare_shared(params, L)
    nc = build_program(T, L, order=unit_order())
    in_maps = []
    for b in range(B):
        m = dict(shared)
        m["x"] = np.ascontiguousarray(x[b])
        m["cT"] = _colvec(c[b])
        in_maps.append(m)
    res = run_bass_kernel_spmd(nc, in_maps, core_ids=list(range(B)))
    return np.stack([np.asarray(r["y"], np.float32) for r in res.results], axis=0)
```
